# Optimizing a Trainium2 kernel written in Bass

```python
import jax, jax.numpy as jnp
from jax import lax
import numpy as np

D_MODEL = 1024
BATCH = 8
SEQ = 2048
DEPTH = 2
DEC_BATCH = 128
DEC_SEQ = 1
PAST_LEN = 16384
PAGE_SIZE = 128

N_MIXERS = 2
N_CONV_LAYERS = (DEPTH + 1) // 2
N_RWKV_LAYERS = DEPTH // 2
N_META = 16
CONV_WIDTH = 31
CONV_BUF = CONV_WIDTH - 1
HEAD_SIZE = 64
N_HEADS = D_MODEL // HEAD_SIZE
D_FF = 4 * D_MODEL
DECAY_LORA = 64
AAA_LORA = 64
GATE_LORA = 128
RMS_EPS = 1e-6
LN_EPS = 1e-5
GN_EPS = 64e-5
L2_EPS = 1e-12

kernel_name = "hybrid_conformer_conv_rwkv7_step"


def _rmsnorm(x, g):
    xf = x.astype(jnp.float32)
    y = xf * lax.rsqrt(jnp.mean(xf * xf, axis=-1, keepdims=True) + RMS_EPS)
    return (y * g.astype(jnp.float32)).astype(x.dtype)


def _conformer_conv(h, buf, w_pw1, b_pw1, w_dw, b_dw, ln_g, ln_b, w_pw2, b_pw2):
    u = h @ w_pw1 + b_pw1
    u = u[..., :D_MODEL] * jax.nn.sigmoid(u[..., D_MODEL:])
    full = jnp.concatenate([buf.astype(u.dtype), u], axis=1)
    c = lax.conv_general_dilated(
        full, w_dw[:, None, :].astype(full.dtype), window_strides=(1,), padding='VALID',
        dimension_numbers=('NWC', 'WIO', 'NWC'), feature_group_count=D_MODEL) + b_dw
    cf = c.astype(jnp.float32)
    mu = jnp.mean(cf, axis=-1, keepdims=True)
    var = jnp.mean(jnp.square(cf - mu), axis=-1, keepdims=True)
    cn = ((cf - mu) * lax.rsqrt(var + LN_EPS) * ln_g.astype(jnp.float32) + ln_b.astype(jnp.float32)).astype(h.dtype)
    out = jax.nn.silu(cn) @ w_pw2 + b_pw2
    return out, full[:, -CONV_BUF:]


def _wkv7_step(S, inp):
    r_t, w_t, k_t, v_t, kk_t, a_t = inp
    sa = jnp.einsum('bhvk,bhk->bhv', S, -kk_t)
    S = (S * w_t[:, :, None, :] + sa[..., None] * (kk_t * a_t)[:, :, None, :]
         + v_t[..., None] * k_t[:, :, None, :])
    y = jnp.einsum('bhvk,bhk->bhv', S, r_t)
    return S, y


def _rwkv7(h, shift_prev, S0, x_mix, w_r, w_k, w_v, w_o, w0, w1, w2, a0, a1, a2,
           g1, g2, k_k, k_a, r_k, gn_g, gn_b):
    B, T, _ = h.shape
    f32 = jnp.float32
    prev = jnp.concatenate([shift_prev[:, None, :].astype(h.dtype), h[:, :-1]], axis=1)
    xx = prev - h
    xr = h + xx * x_mix[0]
    xw = h + xx * x_mix[1]
    xk = h + xx * x_mix[2]
    xv = h + xx * x_mix[3]
    xa = h + xx * x_mix[4]
    xg = h + xx * x_mix[5]
    r = (xr @ w_r).astype(f32)
    k = (xk @ w_k).astype(f32)
    v = (xv @ w_v).astype(f32)
    w_log = -jax.nn.softplus(-(w0 + jnp.tanh(xw @ w1) @ w2).astype(f32)) - 0.5
    decay = jnp.exp(-jnp.exp(w_log))
    a = jax.nn.sigmoid((a0 + (xa @ a1) @ a2).astype(f32))
    g = jax.nn.sigmoid(xg @ g1) @ g2
    hs = (B, T, N_HEADS, HEAD_SIZE)
    kk = (k * k_k.astype(f32)).reshape(hs)
    kk = kk / jnp.maximum(jnp.sqrt(jnp.sum(kk * kk, axis=-1, keepdims=True)), L2_EPS)
    k = k * (1.0 + (a - 1.0) * k_a.astype(f32))
    r, k, v, decay, a = (t.reshape(hs) for t in (r, k, v, decay, a))
    xs = tuple(jnp.moveaxis(t, 1, 0) for t in (r, decay, k, v, kk, a))
    S_final, ys = lax.scan(_wkv7_step, S0.astype(f32), xs)
    y = jnp.moveaxis(ys, 0, 1)
    mu = jnp.mean(y, axis=-1, keepdims=True)
    var = jnp.mean(jnp.square(y - mu), axis=-1, keepdims=True)
    yn = ((y - mu) * lax.rsqrt(var + GN_EPS)).reshape(B, T, D_MODEL)
    yn = yn * gn_g.astype(f32) + gn_b.astype(f32)
    bonus = (jnp.sum(r * k * r_k.astype(f32), axis=-1, keepdims=True) * v).reshape(B, T, D_MODEL)
    out = ((yn + bonus).astype(h.dtype) * g) @ w_o
    return out, h[:, -1], S_final.astype(S0.dtype)


def _sqrelu_mlp(h, w_in, w_out):
    return jnp.square(jax.nn.relu(h @ w_in)) @ w_out


def _trunk(h, conv_bufs, shift_bufs, wkv_states, norm_mix, norm_mlp, norm_final,
           conv_params, rwkv_params, w_mlp_in, w_mlp_out):
    new_conv, new_shift, new_wkv = [], [], []
    for i in range(DEPTH):
        j = i // N_MIXERS
        hn = _rmsnorm(h, norm_mix[i])
        if i % N_MIXERS == 0:
            out, buf = _conformer_conv(hn, conv_bufs[j], *[p[j] for p in conv_params])
            new_conv.append(buf)
        else:
            out, sh, S = _rwkv7(hn, shift_bufs[j], wkv_states[j], *[p[j] for p in rwkv_params])
            new_shift.append(sh)
            new_wkv.append(S)
        h = h + out
        h = h + _sqrelu_mlp(_rmsnorm(h, norm_mlp[i]), w_mlp_in[i], w_mlp_out[i])
    return (_rmsnorm(h, norm_final), jnp.stack(new_conv, 0), jnp.stack(new_shift, 0),
            jnp.stack(new_wkv, 0))


def setup_inputs(seed: int = 0) -> dict:
    key = jax.random.key(seed)
    ks = iter(jax.random.split(key, 64))
    D = D_MODEL
    NC = N_CONV_LAYERS
    NR = N_RWKV_LAYERS

    def nrm(shape, scale):
        return jax.random.normal(next(ks), shape, jnp.float32) * scale

    def uni(shape, lo, hi):
        return jax.random.uniform(next(ks), shape, jnp.float32, lo, hi)

    return {
        "x_prompt": nrm((BATCH, SEQ, D), 1.0),
        "x_sample": nrm((DEC_BATCH, DEC_SEQ, D), 1.0),
        "state_conv": nrm((NC, DEC_BATCH, CONV_BUF, D), 0.5),
        "state_shift": nrm((NR, DEC_BATCH, D), 1.0),
        "state_wkv": nrm((NR, DEC_BATCH, N_HEADS, HEAD_SIZE, HEAD_SIZE), 0.3),
        "meta_tokens": nrm((N_META, D), 1.0),
        "norm_mix": 1.0 + nrm((DEPTH, D), 0.05),
        "norm_mlp": 1.0 + nrm((DEPTH, D), 0.05),
        "norm_final": 1.0 + nrm((D,), 0.05),
        "conv_w_pw1": nrm((NC, D, 2 * D), D ** -0.5),
        "conv_b_pw1": nrm((NC, 2 * D), 0.02),
        "conv_w_dw": nrm((NC, CONV_WIDTH, D), CONV_WIDTH ** -0.5),
        "conv_b_dw": nrm((NC, D), 0.02),
        "conv_ln_g": 1.0 + nrm((NC, D), 0.05),
        "conv_ln_b": nrm((NC, D), 0.02),
        "conv_w_pw2": nrm((NC, D, D), D ** -0.5),
        "conv_b_pw2": nrm((NC, D), 0.02),
        "rwkv_x_mix": uni((NR, 6, D), 0.0, 1.0),
        "rwkv_w_r": nrm((NR, D, D), D ** -0.5),
        "rwkv_w_k": nrm((NR, D, D), D ** -0.5),
        "rwkv_w_v": nrm((NR, D, D), D ** -0.5),
        "rwkv_w_o": nrm((NR, D, D), D ** -0.5),
        "rwkv_w0": uni((NR, D), -3.0, 1.0),
        "rwkv_w1": nrm((NR, D, DECAY_LORA), 0.5 * D ** -0.5),
        "rwkv_w2": nrm((NR, DECAY_LORA, D), 0.5 * DECAY_LORA ** -0.5),
        "rwkv_a0": nrm((NR, D), 0.1),
        "rwkv_a1": nrm((NR, D, AAA_LORA), 0.5 * D ** -0.5),
        "rwkv_a2": nrm((NR, AAA_LORA, D), 0.5 * AAA_LORA ** -0.5),
        "rwkv_g1": nrm((NR, D, GATE_LORA), D ** -0.5),
        "rwkv_g2": nrm((NR, GATE_LORA, D), GATE_LORA ** -0.5),
        "rwkv_k_k": 0.85 + nrm((NR, D), 0.05),
        "rwkv_k_a": 1.0 + nrm((NR, D), 0.05),
        "rwkv_r_k": nrm((NR, N_HEADS, HEAD_SIZE), 0.1),
        "rwkv_gn_g": 1.0 + nrm((NR, D), 0.05),
        "rwkv_gn_b": nrm((NR, D), 0.02),
        "w_mlp_in": nrm((DEPTH, D, D_FF), D ** -0.5),
        "w_mlp_out": nrm((DEPTH, D_FF, D), D_FF ** -0.5),
    }


def reference(x_prompt, x_sample, state_conv, state_shift, state_wkv, meta_tokens,
              norm_mix, norm_mlp, norm_final,
              conv_w_pw1, conv_b_pw1, conv_w_dw, conv_b_dw, conv_ln_g, conv_ln_b,
              conv_w_pw2, conv_b_pw2,
              rwkv_x_mix, rwkv_w_r, rwkv_w_k, rwkv_w_v, rwkv_w_o, rwkv_w0, rwkv_w1, rwkv_w2,
              rwkv_a0, rwkv_a1, rwkv_a2, rwkv_g1, rwkv_g2, rwkv_k_k, rwkv_k_a, rwkv_r_k,
              rwkv_gn_g, rwkv_gn_b,
              w_mlp_in, w_mlp_out):
    conv_params = (conv_w_pw1, conv_b_pw1, conv_w_dw, conv_b_dw, conv_ln_g, conv_ln_b,
                   conv_w_pw2, conv_b_pw2)
    rwkv_params = (rwkv_x_mix, rwkv_w_r, rwkv_w_k, rwkv_w_v, rwkv_w_o, rwkv_w0, rwkv_w1, rwkv_w2,
                   rwkv_a0, rwkv_a1, rwkv_a2, rwkv_g1, rwkv_g2, rwkv_k_k, rwkv_k_a, rwkv_r_k,
                   rwkv_gn_g, rwkv_gn_b)
    B = x_prompt.shape[0]
    dt = x_prompt.dtype
    meta = jnp.broadcast_to(meta_tokens[None].astype(dt), (B, N_META, D_MODEL))
    hp = jnp.concatenate([meta, x_prompt], axis=1)
    zero_conv = jnp.zeros((N_CONV_LAYERS, B, CONV_BUF, D_MODEL), dt)
    zero_shift = jnp.zeros((N_RWKV_LAYERS, B, D_MODEL), dt)
    zero_wkv = jnp.zeros((N_RWKV_LAYERS, B, N_HEADS, HEAD_SIZE, HEAD_SIZE), state_wkv.dtype)
    yp, conv_prompt, shift_prompt, wkv_prompt = _trunk(
        hp, zero_conv, zero_shift, zero_wkv, norm_mix, norm_mlp, norm_final,
        conv_params, rwkv_params, w_mlp_in, w_mlp_out)
    y_prompt = yp[:, N_META:]
    y_sample, conv_sample, shift_sample, wkv_sample = _trunk(
        x_sample, state_conv, state_shift, state_wkv, norm_mix, norm_mlp, norm_final,
        conv_params, rwkv_params, w_mlp_in, w_mlp_out)
    return (y_prompt, y_sample, conv_prompt, shift_prompt, wkv_prompt,
            conv_sample, shift_sample, wkv_sample)
```

```python
import numpy as np
from contextlib import ExitStack
import concourse.bass as bass
import concourse.mybir as mybir
from concourse.bass_utils import run_bass_kernel_spmd

F32 = mybir.dt.float32
BF16 = mybir.dt.bfloat16
AF = mybir.ActivationFunctionType
ALU = mybir.AluOpType
AX = mybir.AxisListType

ENGS = ("pe", "act", "dve", "pool", "sp")
D = 1024
NT = 2112
NG = 704
NGRP = 3
RMS_EPS = 1e-6
LN_EPS = 1e-5
GN_EPS = 64e-5


class Prog:
    def __init__(self, nc):
        self.nc = nc
        self.ops = {e: [] for e in ENGS}
        self.cnt = {e: 0 for e in ENGS}
        self.clock = {e: {} for e in ENGS}
        self.reg = {}
        self.dma_cnt = {}
        self.final = []
        self.pending = {}
        self.defer = None
        self.swdge_hist = []

    def _get(self, k, create=False):
        r = self.reg.get(k)
        if r is None and isinstance(k, tuple) and k[0] in self.pending:
            r = [None, None, list(self.pending[k[0]])]
            self.reg[k] = r
        if r is None and create:
            r = [None, None, []]
            self.reg[k] = r
        return r

    def fence(self, prefix):
        deps = list(self.pending.get(prefix, []))
        for k in list(self.reg.keys()):
            if isinstance(k, tuple) and k[0] == prefix:
                r = self.reg.pop(k)
                if r[0] is not None:
                    deps.append((r[0], r[1]))
                deps.extend(r[2])
        best = {}
        for tok, clk in deps:
            if tok[0] not in best or best[tok[0]][0][1] < tok[1]:
                best[tok[0]] = (tok, clk)
        self.pending[prefix] = list(best.values())

    def schedule(self, flat, DUR):
        import heapq
        units = []
        op2unit = []
        i = 0
        while i < len(flat):
            j = i + 1
            if flat[i][0] == "pe":
                while j < len(flat) and flat[j][0] == "pe" and j - i < 64:
                    j += 1
            d = 0.0
            for op in flat[i:j]:
                d += op[6] if (len(op) > 6 and op[6] is not None) else DUR[op[0]]
            units.append((flat[i][0], flat[i:j], d))
            op2unit += [len(units) - 1] * (j - i)
            i = j
        nU = len(units)
        udeps = [set() for _ in range(nU)]
        reg_ = {}
        for idx, op in enumerate(flat):
            eng, reads, writes = op[0], op[2], op[3]
            u = op2unit[idx]
            for k in reads:
                r = reg_.get(k)
                if r is not None and r[0] is not None and r[0] != u:
                    udeps[u].add(r[0])
                if r is not None and isinstance(k, tuple) and k[0] == "ps":
                    for x in r[1]:
                        if x != u and units[x][0] != eng:
                            udeps[u].add(x)
            for k in writes:
                r = reg_.get(k)
                if r is not None:
                    if r[0] is not None and r[0] != u:
                        udeps[u].add(r[0])
                    for x in r[1]:
                        if x != u:
                            udeps[u].add(x)
            for k in reads:
                reg_.setdefault(k, [None, []])[1].append(u)
            for k in writes:
                reg_[k] = [u, []]
        succ = [[] for _ in range(nU)]
        for u in range(nU):
            for d in udeps[u]:
                succ[d].append(u)
        cp = [0.0] * nU
        for u in range(nU - 1, -1, -1):
            m = 0.0
            for v in succ[u]:
                if cp[v] > m:
                    m = cp[v]
            cp[u] = m + units[u][2] + 0.6
        ndep = [len(udeps[u]) for u in range(nU)]
        ready_t = [0.0] * nU
        heaps = {e: [] for e in ENGS}
        for u in range(nU):
            if ndep[u] == 0:
                heapq.heappush(heaps[units[u][0]], (0.0, -cp[u], u))
        efree = {e: 0.0 for e in ENGS}
        order = []
        done = 0
        while done < nU:
            best = None
            for e in ENGS:
                h = heaps[e]
                if not h:
                    continue
                tfree = efree[e]
                cands = []
                while h and h[0][0] <= tfree:
                    cands.append(heapq.heappop(h))
                if cands:
                    cands.sort(key=lambda x: (x[1], x[2]))
                    pick = cands[0]
                    for c in cands[1:]:
                        heapq.heappush(h, c)
                    start = tfree
                else:
                    pick = heapq.heappop(h)
                    start = pick[0]
                if best is None or start < best[0]:
                    if best is not None:
                        heapq.heappush(heaps[best[2]], best[1])
                    best = (start, pick, e)
                else:
                    heapq.heappush(h, pick)
            start, pick, e = best
            u = pick[2]
            fin = start + units[u][2]
            efree[e] = fin
            order.append((start, u))
            done += 1
            for v in succ[u]:
                lat = 0.1 if units[v][0] == e else 0.6
                if fin + lat > ready_t[v]:
                    ready_t[v] = fin + lat
                ndep[v] -= 1
                if ndep[v] == 0:
                    heapq.heappush(heaps[units[v][0]], (ready_t[v], -cp[v], v))
        order.sort()
        for _, u in order:
            self.replay(units[u][1])
        return max(efree.values())

    def replay(self, ops):
        for op in ops:
            eng, fn, reads, writes, dsem, final = op[:6]
            self.add(eng, fn, reads=reads, writes=writes, dsem=dsem, final=final)

    def add(self, eng, fn, reads=(), writes=(), dsem=None, final=False, dur=None):
        if self.defer is not None:
            self.defer.append((eng, fn, tuple(reads), tuple(writes), dsem, final, dur))
            return None
        deps = []
        for k in reads:
            r = self._get(k)
            if r is not None and r[0] is not None:
                deps.append((r[0], r[1]))
            elif r is not None and r[0] is None and r[2]:
                deps.extend(r[2])
            if r is not None and isinstance(k, tuple) and k[0] == "ps":
                deps.extend([x for x in r[2] if x[0][0] != eng])
        for k in writes:
            r = self._get(k)
            if r is not None:
                if r[0] is not None:
                    deps.append((r[0], r[1]))
                deps.extend(r[2])
        clk = self.clock[eng]
        best = {}
        for (tok, tclk) in deps:
            sk, v = tok
            if sk == eng and eng == "pe":
                continue
            if clk.get(sk, 0) >= v:
                continue
            if best.get(sk, 0) < v:
                best[sk] = v
            for s2, v2 in tclk.items():
                if clk.get(s2, 0) < v2:
                    clk[s2] = v2
            clk[sk] = max(clk.get(sk, 0), v)
        if dsem is not None and eng == "pool":
            hist = self.swdge_hist
            if len(hist) >= 8:
                ptok, pclk = hist[-8]
                if clk.get(ptok[0], 0) < ptok[1]:
                    best[ptok[0]] = max(best.get(ptok[0], 0), ptok[1])
                    for s2, v2 in pclk.items():
                        if clk.get(s2, 0) < v2:
                            clk[s2] = v2
                    clk[ptok[0]] = max(clk.get(ptok[0], 0), ptok[1])
        waits = list(best.items())
        if dsem is None:
            self.cnt[eng] += 1
            tok = (eng, self.cnt[eng])
        else:
            sk = ("d", dsem)
            self.dma_cnt[sk] = self.dma_cnt.get(sk, 0) + 16
            tok = (sk, self.dma_cnt[sk])
        myclk = dict(clk)
        myclk[tok[0]] = max(myclk.get(tok[0], 0), tok[1])
        if dsem is not None and eng == "pool":
            self.swdge_hist.append((tok, myclk))
        self.ops[eng].append((fn, waits, (tok[0], 16 if dsem is not None else 1)))
        for k in reads:
            r = self._get(k, create=True)
            r[2].append((tok, myclk))
        for k in writes:
            self.reg[k] = [tok, myclk, []]
        if final:
            self.final.append(tok)
        return tok

    def emit(self):
        nc = self.nc
        with ExitStack() as st:
            sems = {}
            for e in ENGS:
                sems[e] = st.enter_context(nc.semaphore("s_" + e))
            for sk in self.dma_cnt:
                sems[sk] = st.enter_context(nc.semaphore("s_d%d" % sk[1]))
            block = st.enter_context(nc.Block())
            deco = {"pe": block.tensor, "act": block.scalar, "dve": block.vector,
                    "pool": block.gpsimd, "sp": block.sync}
            final = list(self.final)

            def mk(e):
                def body(eng):
                    for fn, waits, inc in self.ops[e]:
                        for sk, v in waits:
                            eng.wait_ge(sems[sk], v)
                        fn(eng).then_inc(sems[inc[0]], inc[1])
                    if e == "sp":
                        best = {}
                        for sk, v in final:
                            best[sk] = max(best.get(sk, 0), v)
                        for sk, v in best.items():
                            eng.wait_ge(sems[sk], v)
                return body

            for e in ENGS:
                deco[e](mk(e))


R_NMIX, R_NMLP, R_NFIN, R_BPW1, R_WDW, R_BDW, R_LNG, R_LNB, R_BPW2 = 0, 2, 4, 5, 7, 38, 39, 40, 41
R_XMIX, R_W0, R_A0, R_KK, R_KA, R_RK, R_GNG, R_GNB = 42, 48, 49, 50, 51, 52, 53, 54
NVEC = 55


def build(stage=9):
    nc = bass.Bass("TRN2", target_bir_lowering=False)

    def din(name, shape):
        return nc.dram_tensor(name, shape, F32, kind="ExternalInput").ap()

    def dout(name, shape):
        return nc.dram_tensor(name, shape, F32, kind="ExternalOutput").ap()

    xp = din("xp", [2048, D]); xs = din("xs", [16, D]); sconv = din("sconv", [480, D])
    sshift = din("sshift", [16, D]); swkv = din("swkv", [128, 8192]); meta = din("meta", [16, D])
    vecs = din("vecs", [NVEC, D])
    w_pw1 = din("w_pw1", [D, 2 * D]); w_pw2 = din("w_pw2", [D, D])
    w_r = din("w_r", [D, D]); w_k = din("w_k", [D, D]); w_v = din("w_v", [D, D]); w_o = din("w_o", [D, D])
    w1 = din("w1", [D, 64]); w2 = din("w2", [64, D]); a1 = din("a1", [D, 64]); a2 = din("a2", [64, D])
    g1 = din("g1", [D, 128]); g2 = din("g2", [128, D])
    w_in = din("w_in", [2, D, 4 * D]); w_out = din("w_out", [2, 4 * D, D])

    y_p = dout("y_p", [2048, D]); y_s = dout("y_s", [16, D]); o_convp = dout("o_convp", [30, D])
    o_shiftp = dout("o_shiftp", [1, D]); o_wkvp = dout("o_wkvp", [16, 64, 64])
    o_convs = dout("o_convs", [480, D]); o_shifts = dout("o_shifts", [16, D]); o_wkvs = dout("o_wkvs", [128, 8192])

    scr1 = nc.dram_tensor("scr1", [112, D], F32, kind="Internal").ap()
    scr2 = nc.dram_tensor("scr2", [128, 128], F32, kind="Internal").ap()
    st = ExitStack()
    P = Prog(nc)

    def sb(name, shape, dt):
        return st.enter_context(nc.sbuf_tensor(name, shape, dt))

    hT = sb("hT", [128, 8, NG], F32)
    cT = sb("cT", [128, 8, 64], F32)
    cX = sb("cX", [128, 8, 16], F32)
    identf = sb("identf", [128, 128], F32)
    identb = sb("identb", [128, 128], BF16)
    onesb = sb("onesb", [128, 128], BF16)
    blkb = sb("blkb", [128, 128], BF16)
    onesf = sb("onesf", [128, 64], F32)
    m_su = sb("m_su", [128, 8, 64], BF16)
    m_il = sb("m_il", [128, 8, 64], BF16)
    m_sl = sb("m_sl", [128, 8, 64], BF16)
    m_id = sb("m_id", [128, 8, 64], BF16)
    Sst = sb("Sst", [128, 8, 64], F32)
    Sbf = sb("Sbf", [128, 8, 64], BF16)
    gtail = sb("gtail", [128, 8, 30], BF16)
    hlast = sb("hlast", [128, 8, 1], F32)
    shiftT = sb("shiftT", [128, 8, 16], F32)
    g32s = sb("g32s", [128, 8, 16], F32)
    g32p = sb("g32p", [128, 8, 30], F32)
    sampF = sb("sampF", [128, 8, 112], F32)
    CGp = sb("CGp", [128, 3, 128], F32)
    ring = [sb("ring%d" % i, [128, 4096], BF16) for i in range(2)]
    A1 = sb("A1", [128, 38400], BF16)
    A2 = sb("A2", [128, 39040], BF16)
    psall = st.enter_context(nc.psum_tensor("psall", [128, 4096], F32))
    ps = [psall[:, i * 512:(i + 1) * 512] for i in range(8)]

    class Arena:
        def __init__(self, t, name):
            self.t, self.name, self.off = t, name, 0

        def reset(self):
            P.fence(self.name)
            self.off = 0

        def take(self, shape, dt):
            n = int(np.prod(shape[1:]))
            nb = n * (2 if dt == BF16 else 4)
            nb = (nb + 63) // 64 * 64
            a = self.t[:, self.off // 2:(self.off + nb) // 2]
            self.off += nb
            assert self.off <= self.t.shape[1] * 2, (self.name, self.off)
            if dt == F32:
                a = a.bitcast(F32)
            a = a[:, 0:n]
            if len(shape) == 3:
                a = a.rearrange("p (k n) -> p k n", k=shape[1])
            elif len(shape) == 4:
                a = a.rearrange("p (a b c) -> p a b c", a=shape[1], b=shape[2])
            return a

    AR1 = Arena(A1, "A1")
    AR2 = Arena(A2, "A2")

    def cv(row, k=None):
        return cT[:, k, row:row + 1]

    P.add("pool", lambda e: e.memset(identf[:], 0.0), writes=["identf"])
    P.add("pool", lambda e: e.affine_select(out=identf[:], in_=identf[:], pattern=[[-1, 128]],
                                            compare_op=ALU.not_equal, fill=1.0, base=0, channel_multiplier=1),
          reads=["identf"], writes=["identf"])
    P.add("dve", lambda e: e.tensor_copy(out=identb[:], in_=identf[:]), reads=["identf"], writes=["identb"])
    P.add("dve", lambda e: e.memset(onesb[:], 1.0), writes=["onesb"])
    P.add("dve", lambda e: e.memset(onesf[:], 1.0), writes=["onesf"])
    P.add("dve", lambda e: e.memset(blkb[:], 0.0), writes=["blkb"])
    P.add("dve", lambda e: e.memset(blkb[0:64, 0:64], 1.0), reads=["blkb"], writes=["blkb"])
    P.add("dve", lambda e: e.memset(blkb[64:128, 64:128], 1.0), reads=["blkb"], writes=["blkb"])
    AR2.reset()
    onesv = AR2.take([128, 8, 64], F32)
    P.add("dve", lambda e: e.memset(onesv, 1.0), writes=[("A2", "onesv")])
    for (m, nm, op, cm, stp) in ((m_su, "m_su", ALU.is_gt, -1, 1), (m_il, "m_il", ALU.is_ge, -1, 1),
                                 (m_sl, "m_sl", ALU.is_gt, 1, -1), (m_id, "m_id", ALU.is_equal, 1, -1)):
        for h in range(2):
            P.add("pool", lambda e, m=m, op=op, cm=cm, stp=stp, h=h: e.affine_select(
                out=m[h * 64:(h + 1) * 64], in_=onesv[h * 64:(h + 1) * 64], pattern=[[0, 8], [stp, 64]],
                compare_op=op, fill=0.0, base=0, channel_multiplier=cm),
                reads=[("A2", "onesv")], writes=[(nm, h)])
    MASKR = lambda nm: [(nm, 0), (nm, 1)]

    crow = AR2.take([128, 1024], F32)
    P.add("sp", lambda e: e.dma_start(out=crow[0:NVEC, :], in_=vecs[:, :]), writes=[("A2", "crow")], dsem=8)
    for k in range(8):
        P.add("pe", lambda e, k=k: e.transpose(out=ps[0][:, k * 64:k * 64 + NVEC], in_=crow[0:NVEC, k * 128:(k + 1) * 128],
                                               identity=identf[0:NVEC, 0:NVEC]),
              reads=[("A2", "crow"), "identf"], writes=[("ps", 0)])
    P.add("dve", lambda e: e.tensor_copy(out=cT[:, :, 0:NVEC], in_=ps[0][:].rearrange("p (k n) -> p k n", k=8)[:, :, 0:NVEC]),
          reads=[("ps", 0)], writes=["cT"])
    P.add("dve", lambda e: e.tensor_scalar(out=cX[:, :, 0:6], in0=cT[:, :, R_XMIX:R_XMIX + 6], scalar1=-1.0, scalar2=1.0,
                                           op0=ALU.mult, op1=ALU.add), reads=["cT"], writes=["cX"])
    P.add("dve", lambda e: e.tensor_scalar(out=cX[:, :, 6:7], in0=cT[:, :, R_W0:R_W0 + 1], scalar1=-1.0, scalar2=None,
                                           op0=ALU.mult), reads=["cT", "cX"], writes=["cX"])
    for ci, row in enumerate((R_GNG, R_GNB, R_RK)):
        for t in range(16):
            P.add("sp", lambda e, t=t, ci=ci, row=row: e.dma_start(out=CGp[t * 8:(t + 1) * 8, ci, :], in_=vecs[row].rearrange("(j e) -> j e", e=128)),
                  writes=[("CGp", ci, t)], dsem=23)
    _l = P.reg[("CGp", 2, 15)]
    P.reg["CGp"] = [_l[0], _l[1], []]
    P.add("dve", lambda e: e.memset(Sst[:], 0.0), writes=["Sst"])
    P.add("dve", lambda e: e.memset(Sbf[:], 0.0), writes=["Sbf"])
    P.add("dve", lambda e: e.memset(gtail[:], 0.0), writes=["gtail"])
    P.add("dve", lambda e: e.memset(hlast[:], 0.0), writes=["hlast"])

    rr = {"i": 0}

    def ring_fill(dmas):
        s = rr["i"] % 2
        rr["i"] += 1
        for di, mk in enumerate(dmas):
            o, i = mk(ring[s])
            wk = [("ring", s)] if di == 0 else [("ringg", s)]
            if len(dmas) == 1:
                wk = [("ring", s), ("ringg", s)]
            P.add("pool", lambda e, o=o, i=i: e.dma_start(out=o, in_=i), writes=wk, dsem=s if di == 0 else 2 + s)
        return s

    def tiles_of(n, step=352):
        out, c = [], 0
        while c < n:
            m = min(step, n - c)
            out.append((c, m))
            c += m
        return out

    TL = tiles_of(NG)
    psr = {"lo": 0, "hi": 8}
    psc = {}

    def next_ps():
        key = (psr["lo"], psr["hi"])
        c = psc.get(key, 0)
        psc[key] = c + 1
        return key[0] + c % (key[1] - key[0])

    def rmsnorm(src, n, grow, dst, tmp, tmpkey, extra_reads=(), srckey="hT", tmps=None, lnexp=False, tilekeys=False):
        if tmps is None:
            sq = tmp.take([128, 2, 352], BF16)
            rs = tmp.take([128, 352], F32)
        else:
            sq, rs = tmps
        for (t0, m) in tiles_of(n):
            b = next_ps()
            for k in range(8):
                P.add("act", lambda e, k=k, t0=t0, m=m: e.activation(out=sq[:, k % 2, 0:m], in_=src[:, k, t0:t0 + m], func=AF.Square),
                      reads=[srckey], writes=[(tmpkey, "sq", k % 2)])
                P.add("pe", lambda e, k=k, m=m, b=b: e.matmul(ps[b][:, 0:m], lhsT=onesb[:], rhs=sq[:, k % 2, 0:m], start=(k == 0), stop=(k == 7)),
                      reads=[(tmpkey, "sq", k % 2), "onesb"], writes=[("ps", b)])
            if lnexp:
                P.add("act", lambda e, m=m, b=b: e.activation(out=rs[:, 0:m], in_=ps[b][:, 0:m], func=AF.Ln, scale=1.0 / D, bias=RMS_EPS),
                      reads=[("ps", b)], writes=[(tmpkey, "rs")])
                P.add("act", lambda e, m=m: e.activation(out=rs[:, 0:m], in_=rs[:, 0:m], func=AF.Exp, scale=-0.5), reads=[(tmpkey, "rs")], writes=[(tmpkey, "rs")])
            else:
                P.add("act", lambda e, m=m, b=b: e.activation(out=rs[:, 0:m], in_=ps[b][:, 0:m], func=AF.Sqrt, scale=1.0 / D, bias=RMS_EPS),
                      reads=[("ps", b)], writes=[(tmpkey, "rs")])
                P.add("dve", lambda e, m=m: e.reciprocal(out=rs[:, 0:m], in_=rs[:, 0:m]), reads=[(tmpkey, "rs")], writes=[(tmpkey, "rs")])
            for k in range(8):
                P.add("dve", lambda e, k=k, t0=t0, m=m: e.scalar_tensor_tensor(
                    out=dst[:, k, t0:t0 + m], in0=src[:, k, t0:t0 + m], scalar=cv(grow, k), in1=rs[:, 0:m], op0=ALU.mult, op1=ALU.mult),
                    reads=[srckey, (tmpkey, "rs"), "cT"], writes=[(tmpkey, "dst", k, t0) if tilekeys else (tmpkey, "dst", k)])

    wv = lambda w: w.rearrange("(k p) n -> p k n", p=128)


    def rwkv(g):
        c0 = g * NG
        AR1.reset()
        AR2.reset()
        Wr = AR1.take([128, 8, 1024], BF16); Wk = AR1.take([128, 8, 1024], BF16)
        Wv = AR1.take([128, 8, 1024], BF16); Wo = AR1.take([128, 8, 1024], BF16)
        W1 = AR1.take([128, 8, 64], BF16); A1w = AR1.take([128, 8, 64], BF16); G1 = AR1.take([128, 8, 128], BF16)
        W2 = AR1.take([128, 1024], BF16); A2w = AR1.take([128, 1024], BF16); G2 = AR1.take([128, 1024], BF16)
        wl = [(Wr, wv(w_r), "Wr"), (Wk, wv(w_k), "Wk"), (Wv, wv(w_v), "Wv"), (W1, wv(w1), "W1"), (W2[0:64, :], w2[:, :], "W2"),
              (A1w, wv(a1), "A1w"), (A2w[0:64, :], a2[:, :], "A2w"), (G1, wv(g1), "G1"), (G2[:, :], g2[:, :], "G2"), (Wo, wv(w_o), "Wo")]
        for wi, (dst_, src_, nm) in enumerate(wl):
            P.add("pool", lambda e, d=dst_, s_=src_: e.dma_start(out=d, in_=s_), writes=[("A1", nm)], dsem=30 + wi)

        T = 64
        n = 64
        f32 = lambda nm, *sh: AR2.take([128] + list(sh), F32)
        b16 = lambda nm, *sh: AR2.take([128] + list(sh), BF16)
        hnw = f32("hnw", 8, T + 1); xx = f32("xx", 8, T)
        xm = [b16("xm", 8, T) for _ in range(2)]
        vT = b16("vT", 8, T); a32 = f32("a32", 8, T)
        rgf = lambda s_, o_: ring[s_][:, o_ * 1024:(o_ + 1) * 1024].bitcast(F32).rearrange("p (k n) -> p k n", k=8)
        rgb = lambda s_, o_: ring[s_][:, o_:o_ + 512].rearrange("p (k n) -> p k n", k=8)
        IF1 = [dict(r32=f32("r32", 8, T), k32=f32("k32", 8, T), kk32=f32("kk32", 8, T), b32=f32("b32", 8, T), ew=f32("ew", 8, T)),
               dict(r32=rgf(0, 0), k32=rgf(0, 1), kk32=rgf(0, 2), b32=rgf(0, 3), ew=rgf(1, 0))]
        tmpA = f32("tmpA", 8, T); tmpB = b16("tmpB", 8, T)
        lo1 = b16("lo1", T); lo1f = f32("lo1f", T)
        cs = f32("cs", 8, 64); e1 = f32("e1", 8, 64); e2 = f32("e2", 8, 64); d1 = f32("d1", 8, 64)
        kt = b16("kt", 8, 64); bt = b16("bt", 8, 64); kPC = b16("kPC", 8, 64); bPC = b16("bPC", 8, 64)
        Mb = [b16("M", 8, 64) for _ in range(2)]; MTb = [b16("MT", 8, 64) for _ in range(2)]; Rb = [b16("R", 8, 64) for _ in range(2)]
        nsq = AR2.take([128, 2, 64], BF16); nrs = AR2.take([128, 64], F32)
        IF = []
        for i_ in range(2):
            IF.append(dict(rt=b16("rt", 8, 64), at=b16("at", 8, 64), Aak=b16("Aak", 8, 64), Arb=b16("Arb", 8, 64), Ark=b16("Ark", 8, 64),
                           RF=b16("RF", 8, 64), kPCt=b16("kPCt", 8, 64), bPCt=b16("bPCt", 8, 64), PCc=f32("PCc", 8, 1)))
        IF3 = [dict(Vtok=b16("Vtok", 8, 64), gT=b16("gT", 8, 64), bonT=f32("bonT", 8, 64)) for _ in range(2)]
        IF3.append(dict(Vtok=rgb(1, 1024), gT=rgb(1, 1536), bonT=rgf(1, 2)))
        Wb = b16("Wb", 8, 64); Ub = b16("Ub", 8, 64)
        scrS = f32("scrS", 1024)
        ysq = scrS[:, 0:512].rearrange("p (k n) -> p k n", k=8); yc = scrS[:, 512:1024].rearrange("p (k n) -> p k n", k=8)
        yh = b16("yh", 8, 64); z1 = f32("z1", 8, 64); st8 = f32("st8", 8, 8); zT = b16("zT", 8, T)
        K2 = lambda nm: ("A2", nm)
        v3 = lambda ap, k=8: ap.rearrange("p (k n) -> p k n", k=k)
        if g == 0:
            print("rwkv A2 bytes used", AR2.off, "of", A2.shape[1] * 2)

        def ev(eng, out, in_, reads, writes):
            if eng == "act":
                P.add("act", lambda e: e.activation(out=out, in_=in_, func=AF.Copy), reads=reads, writes=writes)
            else:
                P.add(eng, lambda e: e.tensor_copy(out=out, in_=in_), reads=reads, writes=writes)

        def tt(eng, out, in0, in1, op, reads, writes):
            P.add(eng, lambda e: e.tensor_tensor(out=out, in0=in0, in1=in1, op=op), reads=reads, writes=writes)

        def act(out, in_, func, reads, writes, **kw):
            P.add("act", lambda e: e.activation(out=out, in_=in_, func=func, **kw), reads=reads, writes=writes)

        def proj(Wt, wkey, xin, xkey):
            b = next_ps()
            for o in range(8):
                for k in range(8):
                    P.add("pe", lambda e, b=b, o=o, k=k: e.matmul(ps[b][:, o * 64:(o + 1) * 64], lhsT=Wt[:, k, o * 128:(o + 1) * 128],
                                                                  rhs=xin[:, k, 0:n], start=(k == 0), stop=(k == 7)),
                          reads=[wkey, xkey], writes=[("ps", b)])
            return b

        def blockmm(b, pairs, reads):
            for par in range(2):
                for j in range(8):
                    for pi, (L, Rr) in enumerate(pairs):
                        P.add("pe", lambda e, j=j, par=par, L=L, Rr=Rr, pi=pi: e.matmul(
                            ps[b][par * 64:(par + 1) * 64, j * 64:(j + 1) * 64], lhsT=L[par * 64:(par + 1) * 64, j, :],
                            rhs=Rr[par * 64:(par + 1) * 64, j, :], start=(pi == 0), stop=(pi == len(pairs) - 1)),
                            reads=reads, writes=[("ps", b)])

        tl = tiles_of(NG, T)
        NTI = len(tl)
        base = P.reg.get("hT")
        for ti in range(NTI):
            if base is not None:
                P.reg[("hT", ti)] = [base[0], base[1], list(base[2])]

        def gen_P1(ti):
            tc0 = tl[ti][0]
            I = IF[ti % 2]
            IK = lambda nm: K2((nm, ti % 2))
            J = IF1[ti % 2]
            JK = lambda nm: K2((nm, "j", ti % 2))
            Q = IF3[ti % 3]
            QK = lambda nm: K2((nm, "q", ti % 3))
            HK = ("hT", ti)
            first = (g == 0 and ti == 0)
            rt, at, Aak, Arb, Ark, RF, kPCt, bPCt, PCc = (I[x] for x in ("rt", "at", "Aak", "Arb", "Ark", "RF", "kPCt", "bPCt", "PCc"))
            r32, k32, kk32, b32, ew = (J[x] for x in ("r32", "k32", "kk32", "b32", "ew"))
            Vtok, gT, bonT = (Q[x] for x in ("Vtok", "gT", "bonT"))
            P.add("pool", lambda e: e.tensor_copy(out=hnw[:, :, 0:1], in_=hlast[:]), reads=["hlast"], writes=[K2("hnw")])
            rmsnorm(hT[:, :, tc0:tc0 + n], n, R_NMIX + 1, hnw[:, :, 1:1 + n], AR2, "A2n", tmps=(nsq, nrs), srckey=HK, lnexp=True)
            HNW = [K2("hnw")] + [("A2n", "dst", k) for k in range(8)]
            if first:
                P.add("dve", lambda e: e.memset(hnw[:, :, 17:49], 0.0), reads=HNW, writes=HNW)
            P.add("pool", lambda e: e.tensor_copy(out=hlast[:], in_=hnw[:, :, n:n + 1]), reads=HNW, writes=["hlast"])
            if first:
                for h in range(2):
                    b = next_ps()
                    for kk_ in range(4):
                        k = h * 4 + kk_
                        P.add("pe", lambda e, k=k, kk_=kk_, b=b: e.transpose(out=ps[b][0:16, kk_ * 128:(kk_ + 1) * 128], in_=hnw[:, k, 1:17], identity=identf[:]),
                              reads=HNW + ["identf"], writes=[("ps", b)])
                    ev("act", scrS[0:16, h * 512:(h + 1) * 512], ps[b][0:16, :], [("ps", b)], [K2("scrS")])
                P.add("sp", lambda e: e.dma_start(out=o_shifts[:, :], in_=scrS[0:16, :]), reads=[K2("scrS")], writes=["o_shifts"], dsem=14, final=True)
            if g == NGRP - 1 and ti == NTI - 1:
                P.add("sp", lambda e: e.dma_start(out=o_shiftp.rearrange("o (k p) -> p (o k)", p=128), in_=hnw[:, :, n:n + 1].rearrange("p k o -> p (k o)"),
                                                  allow_slow_non_contiguous=True), reads=HNW, writes=["o_shiftp"], dsem=14, final=True)
            tt("pool", xx[:, :, 0:n], hnw[:, :, 0:n], hnw[:, :, 1:1 + n], ALU.subtract, HNW, [K2("xx")])
            if first:
                tt("pool", xx[:, :, 0:16], shiftT[:], hnw[:, :, 1:17], ALU.subtract, HNW + ["shiftT", K2("xx")], [K2("xx")])
            mixi = {"r": 0, "w": 1, "k": 2, "v": 3, "a": 4, "g": 5}
            xcnt = {"i": 0}

            def mix(nm):
                i = xcnt["i"] % 2
                xcnt["i"] += 1
                m = mixi[nm]
                tt("pool", tmpA[:, :, 0:n], xx[:, :, 0:n], cT[:, :, R_XMIX + m:R_XMIX + m + 1].to_broadcast([128, 8, n]), ALU.mult,
                   [K2("xx"), "cT"], [K2("tmpA")])
                tt("dve", xm[i][:, :, 0:n], tmpA[:, :, 0:n], hnw[:, :, 1:1 + n], ALU.add, [K2("tmpA")] + HNW, [K2(("xm", i))])
                return xm[i], K2(("xm", i))

            xr, xrk = mix("r")
            b = proj(Wr, ("A1", "Wr"), xr, xrk)
            ev("act", r32, v3(ps[b][:]), [("ps", b)], [JK("r32")])
            xk_, xkk = mix("k")
            b = proj(Wk, ("A1", "Wk"), xk_, xkk)
            ev("act", k32, v3(ps[b][:]), [("ps", b)], [JK("k32")])
            xv, xvk = mix("v")
            Wv4 = Wv.rearrange("p k (j a c) -> p k j a c", a=2, c=64)
            b = next_ps()
            for par in range(2):
                for k in range(8):
                    P.add("pe", lambda e, b=b, par=par, k=k: e.matmul(
                        ps[b][par * 64:(par + 1) * 64, :], lhsT=xv[:, k, 0:64], rhs=Wv4[:, k, :, par, :],
                        start=(k == 0), stop=(k == 7)), reads=[("A1", "Wv"), xvk], writes=[("ps", b)])
            ev("act", Vtok, v3(ps[b][:]), [("ps", b)], [QK("Vtok")])
            bV = next_ps()
            pV = ps[bV][:].bitcast(BF16)
            for par in range(2):
                for j in range(8):
                    P.add("pe", lambda e, j=j, par=par: e.transpose(
                        out=pV[par * 64:(par + 1) * 64, j * 64:(j + 1) * 64], in_=Vtok[par * 64:(par + 1) * 64, j, :],
                        identity=identb[par * 64:(par + 1) * 64, par * 64:(par + 1) * 64]), reads=[QK("Vtok"), "identb"], writes=[("ps", bV)])
            ev("act", vT, v3(pV[:, 0:512]), [("ps", bV)], [K2("vT")])
            if first:
                P.add("dve", lambda e: e.memset(Vtok[0:48], 0.0), reads=[QK("Vtok")], writes=[QK("Vtok")])
                P.add("dve", lambda e: e.memset(Vtok[64:112], 0.0), reads=[QK("Vtok")], writes=[QK("Vtok")])

            def lora(nm, W1t, w1key, W2t, w2key, rank, func1):
                xin, xkey = mix(nm)
                b = next_ps()
                for k in range(8):
                    P.add("pe", lambda e, b=b, k=k: e.matmul(ps[b][0:rank, 0:n], lhsT=W1t[:, k, :], rhs=xin[:, k, 0:n], start=(k == 0), stop=(k == 7)),
                          reads=[w1key, xkey], writes=[("ps", b)])
                if func1 == "tanh":
                    act(lo1f[0:rank, 0:n], ps[b][0:rank, 0:n], AF.Exp, [("ps", b)], [K2("lo1f")], scale=-2.0)
                    act(lo1f[0:rank, 0:n], lo1f[0:rank, 0:n], AF.Ln, [K2("lo1f")], [K2("lo1f")], bias=1.0)
                    act(lo1f[0:rank, 0:n], lo1f[0:rank, 0:n], AF.Exp, [K2("lo1f")], [K2("lo1f")], scale=-1.0)
                    P.add("dve", lambda e: e.tensor_scalar(out=lo1[0:rank, 0:n], in0=lo1f[0:rank, 0:n], scalar1=2.0, scalar2=-1.0, op0=ALU.mult, op1=ALU.add),
                          reads=[K2("lo1f")], writes=[K2("lo1")])
                elif func1 == "sigmoid":
                    act(lo1f[0:rank, 0:n], ps[b][0:rank, 0:n], AF.Exp, [("ps", b)], [K2("lo1f")], scale=-1.0)
                    act(lo1f[0:rank, 0:n], lo1f[0:rank, 0:n], AF.Ln, [K2("lo1f")], [K2("lo1f")], bias=1.0)
                    act(lo1[0:rank, 0:n], lo1f[0:rank, 0:n], AF.Exp, [K2("lo1f")], [K2("lo1")], scale=-1.0)
                else:
                    act(lo1[0:rank, 0:n], ps[b][0:rank, 0:n], func1, [("ps", b)], [K2("lo1")])
                b2 = next_ps()
                for o in range(8):
                    P.add("pe", lambda e, b2=b2, o=o: e.matmul(ps[b2][:, o * 64:(o + 1) * 64], lhsT=W2t[0:rank, o * 128:(o + 1) * 128],
                                                               rhs=lo1[0:rank, 0:n], start=True, stop=True),
                          reads=[w2key, K2("lo1")], writes=[("ps", b2)])
                return b2

            b = lora("w", W1, ("A1", "W1"), W2, ("A1", "W2"), 64, "tanh")
            tt("dve", ew, v3(ps[b][:]), cT[:, :, R_W0:R_W0 + 1].to_broadcast([128, 8, n]), ALU.add, [("ps", b), "cT"], [JK("ew")])
            act(ew, ew, AF.Exp, [JK("ew")], [JK("ew")], scale=-1.0)
            act(ew, ew, AF.Ln, [JK("ew")], [JK("ew")], bias=1.0)
            act(ew, ew, AF.Exp, [JK("ew")], [JK("ew")], scale=-1.0, bias=-0.5)
            b = lora("a", A1w, ("A1", "A1w"), A2w, ("A1", "A2w"), 64, AF.Copy)
            tt("dve", a32, v3(ps[b][:]), cT[:, :, R_A0:R_A0 + 1].to_broadcast([128, 8, n]), ALU.add, [("ps", b), "cT"], [K2("a32")])
            act(a32, a32, AF.Exp, [K2("a32")], [K2("a32")], scale=-1.0)
            act(a32, a32, AF.Ln, [K2("a32")], [K2("a32")], bias=1.0)
            act(a32, a32, AF.Exp, [K2("a32")], [K2("a32")], scale=-1.0)
            b = lora("g", G1, ("A1", "G1"), G2, ("A1", "G2"), 128, "sigmoid")
            ev("act", gT, v3(ps[b][:]), [("ps", b)], [QK("gT")])
            tt("pool", kk32, k32, cT[:, :, R_KK:R_KK + 1].to_broadcast([128, 8, n]), ALU.mult, [JK("k32"), "cT"], [JK("kk32")])
            act(tmpB, kk32, AF.Square, [JK("kk32")], [K2("tmpB")])
            b = next_ps()
            for o in range(8):
                P.add("pe", lambda e, b=b, o=o: e.matmul(ps[b][:, o * 64:(o + 1) * 64], lhsT=blkb[:], rhs=tmpB[:, o, :], start=True, stop=True),
                      reads=["blkb", K2("tmpB")], writes=[("ps", b)])
            P.add("dve", lambda e, b=b: e.tensor_scalar(out=tmpA, in0=v3(ps[b][:]), scalar1=1e-18, scalar2=None, op0=ALU.max), reads=[("ps", b), K2("tmpA")], writes=[K2("tmpA")])
            act(tmpA, tmpA, AF.Ln, [K2("tmpA")], [K2("tmpA")])
            act(tmpA, tmpA, AF.Exp, [K2("tmpA")], [K2("tmpA")], scale=-0.5)
            tt("dve", kk32, kk32, tmpA, ALU.mult, [JK("kk32"), K2("tmpA")], [JK("kk32")])
            P.add("pool", lambda e: e.tensor_scalar(out=tmpA, in0=a32, scalar1=1.0, scalar2=-1.0, op0=ALU.mult, op1=ALU.add), reads=[K2("a32"), K2("tmpA")], writes=[K2("tmpA")])
            tt("pool", tmpA, tmpA, cT[:, :, R_KA:R_KA + 1].to_broadcast([128, 8, n]), ALU.mult, [K2("tmpA"), "cT"], [K2("tmpA")])
            P.add("dve", lambda e: e.scalar_tensor_tensor(out=k32, in0=tmpA, scalar=1.0, in1=k32, op0=ALU.add, op1=ALU.mult),
                  reads=[K2("tmpA"), JK("k32"), JK("kk32")], writes=[JK("k32")])
            tt("pool", b32, kk32, a32, ALU.mult, [JK("kk32"), K2("a32")], [JK("b32")])
            tt("pool", tmpA, r32, cT[:, :, R_RK:R_RK + 1].to_broadcast([128, 8, n]), ALU.mult, [JK("r32"), K2("tmpA"), "cT"], [K2("tmpA")])
            tt("dve", tmpB, tmpA, k32, ALU.mult, [K2("tmpA"), JK("k32"), K2("tmpB")], [K2("tmpB")])
            b = next_ps()
            for o in range(8):
                P.add("pe", lambda e, b=b, o=o: e.matmul(ps[b][:, o * 64:(o + 1) * 64], lhsT=blkb[:], rhs=tmpB[:, o, :], start=True, stop=True),
                      reads=["blkb", K2("tmpB")], writes=[("ps", b)])
            tt("dve", bonT, v3(ps[b][:]), vT, ALU.mult, [("ps", b), K2("vT")], [QK("bonT")])
            if first:
                for mi, (src_, key_) in enumerate(((kk32, JK("kk32")), (None, None), (b32, JK("b32")), (vT, K2("vT")), (k32, JK("k32")), (r32, JK("r32")), (gT, QK("gT")))):
                    if src_ is None:
                        act(sampF[:, :, 16:32], ew[:, :, 0:16], AF.Exp, [JK("ew")], ["sampF"], scale=-1.0)
                    else:
                        P.add("pool", lambda e, mi=mi, src_=src_: e.tensor_copy(out=sampF[:, :, mi * 16:(mi + 1) * 16], in_=src_[:, :, 0:16]),
                              reads=[key_], writes=["sampF"])
        def gen_P2(ti):
            tc0 = tl[ti][0]
            I = IF[ti % 2]
            IK = lambda nm: K2((nm, ti % 2))
            J = IF1[ti % 2]
            JK = lambda nm: K2((nm, "j", ti % 2))
            Q = IF3[ti % 3]
            QK = lambda nm: K2((nm, "q", ti % 3))
            HK = ("hT", ti)
            first = (g == 0 and ti == 0)
            rt, at, Aak, Arb, Ark, RF, kPCt, bPCt, PCc = (I[x] for x in ("rt", "at", "Aak", "Arb", "Ark", "RF", "kPCt", "bPCt", "PCc"))
            r32, k32, kk32, b32, ew = (J[x] for x in ("r32", "k32", "kk32", "b32", "ew"))
            Vtok, gT, bonT = (Q[x] for x in ("Vtok", "gT", "bonT"))
            for j in range(8):
                P.add("dve", lambda e, j=j: e.tensor_tensor_scan(out=cs[:, j, :], data0=onesf[:, 0:64], data1=ew[:, j, :], initial=0.0,
                                                                 op0=ALU.mult, op1=ALU.add), reads=[JK("ew"), "onesf"], writes=[K2("cs")])
            act(e1, cs, AF.Exp, [K2("cs")], [K2("e1")], scale=-1.0)
            tt("dve", rt, r32, e1, ALU.mult, [JK("r32"), K2("e1")], [IK("rt")])
            act(e2, cs, AF.Exp, [K2("cs")], [K2("e2")])
            tt("dve", kt, k32, e2, ALU.mult, [JK("k32"), K2("e2")], [K2("kt")])
            tt("dve", bt, b32, e2, ALU.mult, [JK("b32"), K2("e2")], [K2("bt")])
            tt("pool", d1, cs, ew, ALU.subtract, [K2("cs"), JK("ew")], [K2("d1")])
            act(e1, d1, AF.Exp, [K2("d1"), K2("e1")], [K2("e1")], scale=-1.0)
            P.add("dve", lambda e: e.scalar_tensor_tensor(out=at, in0=kk32, scalar=-1.0, in1=e1, op0=ALU.mult, op1=ALU.mult),
                  reads=[JK("kk32"), K2("e1")], writes=[IK("at")])
            tt("pool", d1, cs, cs[:, :, 63:64].to_broadcast([128, 8, 64]), ALU.subtract, [K2("cs"), K2("d1")], [K2("d1")])
            act(e2, d1, AF.Exp, [K2("d1"), K2("e2")], [K2("e2")])
            tt("pool", kPC, k32, e2, ALU.mult, [JK("k32"), K2("e2")], [K2("kPC")])
            tt("pool", bPC, b32, e2, ALU.mult, [JK("b32"), K2("e2")], [K2("bPC")])
            act(PCc, cs[:, :, 63:64], AF.Exp, [K2("cs")], [IK("PCc")], scale=-1.0)
            if first:
                for (t_, key_, eng_) in ((rt, IK("rt"), "dve"), (kt, K2("kt"), "dve"), (at, IK("at"), "dve"), (bt, K2("bt"), "dve"),
                                         (kPC, K2("kPC"), "pool"), (bPC, K2("bPC"), "pool")):
                    P.add(eng_, lambda e, t_=t_: e.memset(t_[:, :, 0:48], 0.0), reads=[key_], writes=[key_])
            bT = next_ps()
            pT = ps[bT][:].bitcast(BF16)
            for ui, (src_, nm) in enumerate(((kPC, "kPC"), (bPC, "bPC"))):
                for par in range(2):
                    for j in range(8):
                        P.add("pe", lambda e, ui=ui, src_=src_, j=j, par=par: e.transpose(
                            out=pT[par * 64:(par + 1) * 64, ui * 512 + j * 64:ui * 512 + (j + 1) * 64], in_=src_[par * 64:(par + 1) * 64, j, :],
                            identity=identb[par * 64:(par + 1) * 64, par * 64:(par + 1) * 64]), reads=[K2(nm), "identb"], writes=[("ps", bT)])
            ev("act", kPCt, v3(pT[:, 0:512]), [("ps", bT)], [IK("kPCt")])
            ev("act", bPCt, v3(pT[:, 512:1024]), [("ps", bT)], [IK("bPCt")])
            bN = next_ps(); blockmm(bN, [(bt, at)], [K2("bt"), IK("at")])
            tt("dve", Mb[0], v3(ps[bN][:]), m_su[:], ALU.mult, [("ps", bN)] + MASKR("m_su"), [K2(("M", 0))])
            bNT = next_ps(); blockmm(bNT, [(at, bt)], [K2("bt"), IK("at")])
            tt("dve", MTb[0], v3(ps[bNT][:]), m_sl[:], ALU.mult, [("ps", bNT)] + MASKR("m_sl"), [K2(("MT", 0))])
            b_ = next_ps(); blockmm(b_, [(kt, at)], [K2("kt"), IK("at")])
            tt("dve", Aak, v3(ps[b_][:]), m_su[:], ALU.mult, [("ps", b_)] + MASKR("m_su"), [IK("Aak")])
            b_ = next_ps(); blockmm(b_, [(bt, rt)], [K2("bt"), IK("rt")])
            tt("dve", Arb, v3(ps[b_][:]), m_il[:], ALU.mult, [("ps", b_)] + MASKR("m_il"), [IK("Arb")])
            b_ = next_ps(); blockmm(b_, [(kt, rt)], [K2("kt"), IK("rt")])
            tt("dve", Ark, v3(ps[b_][:]), m_il[:], ALU.mult, [("ps", b_)] + MASKR("m_il"), [IK("Ark")])
            tt("pool", Rb[0], Mb[0], m_id[:], ALU.add, [K2(("M", 0))] + MASKR("m_id"), [K2(("R", 0))])
            cur = 0
            for lvl in range(1, 6):
                nxt = 1 - cur
                if lvl < 5:
                    b_ = next_ps(); blockmm(b_, [(MTb[cur], Mb[cur])], [K2(("M", cur)), K2(("MT", cur))])
                    ev("act", Mb[nxt], v3(ps[b_][:]), [("ps", b_)], [K2(("M", nxt))])
                b_ = next_ps(); blockmm(b_, [(Mb[cur], MTb[cur])], [K2(("M", cur)), K2(("MT", cur))])
                ev("act", MTb[nxt], v3(ps[b_][:]), [("ps", b_)], [K2(("MT", nxt))])
                b_ = next_ps(); blockmm(b_, [(MTb[nxt], Rb[cur])], [K2(("MT", nxt)), K2(("R", cur))])
                if lvl < 5:
                    tt("dve", Rb[nxt], v3(ps[b_][:]), Rb[cur], ALU.add, [("ps", b_), K2(("R", cur))], [K2(("R", nxt))])
                else:
                    tt("dve", RF, v3(ps[b_][:]), Rb[cur], ALU.add, [("ps", b_), K2(("R", cur))], [IK("RF")])
                cur = nxt

        def gen_S(ti):
            tc0 = tl[ti][0]
            I = IF[ti % 2]
            IK = lambda nm: K2((nm, ti % 2))
            J = IF1[ti % 2]
            JK = lambda nm: K2((nm, "j", ti % 2))
            Q = IF3[ti % 3]
            QK = lambda nm: K2((nm, "q", ti % 3))
            HK = ("hT", ti)
            first = (g == 0 and ti == 0)
            rt, at, Aak, Arb, Ark, RF, kPCt, bPCt, PCc = (I[x] for x in ("rt", "at", "Aak", "Arb", "Ark", "RF", "kPCt", "bPCt", "PCc"))
            r32, k32, kk32, b32, ew = (J[x] for x in ("r32", "k32", "kk32", "b32", "ew"))
            Vtok, gT, bonT = (Q[x] for x in ("Vtok", "gT", "bonT"))
            VK = QK("Vtok")
            bW = next_ps(); blockmm(bW, [(at, Sbf), (Aak, Vtok)], [IK("at"), "Sbf", IK("Aak"), VK])
            ev("act", Wb, v3(ps[bW][:]), [("ps", bW)], [K2("Wb")])
            bU = next_ps(); blockmm(bU, [(RF, Wb)], [IK("RF"), K2("Wb")])
            ev("act", Ub, v3(ps[bU][:]), [("ps", bU)], [K2("Ub")])
            bY = next_ps(); blockmm(bY, [(rt, Sbf), (Arb, Ub), (Ark, Vtok)], [IK("rt"), "Sbf", IK("Arb"), K2("Ub"), IK("Ark"), VK])
            bS = next_ps(); blockmm(bS, [(bPCt, Ub), (kPCt, Vtok)], [IK("bPCt"), K2("Ub"), IK("kPCt"), VK])
            tt("pool", Sst[:], Sst[:], PCc.to_broadcast([128, 8, 64]), ALU.mult, ["Sst", IK("PCc")], ["Sst"])
            tt("dve", Sst[:], v3(ps[bS][:]), Sst[:], ALU.add, [("ps", bS), "Sst"], ["Sst"])
            ev("act", Sbf[:], Sst[:], ["Sst"], ["Sbf"])
            pY = v3(ps[bY][:])
            SK = K2("scrS")
            P.add("dve", lambda e, pY=pY: e.tensor_reduce(out=st8[:, :, 0], in_=pY, axis=AX.X, op=ALU.add), reads=[("ps", bY)], writes=[K2("st8")])
            act(ysq, pY, AF.Square, [("ps", bY)], [SK])
            P.add("dve", lambda e: e.tensor_reduce(out=st8[:, :, 1], in_=ysq, axis=AX.X, op=ALU.add), reads=[SK, K2("st8")], writes=[K2("st8")])
            P.add("dve", lambda e: e.tensor_scalar(out=st8[:, :, 0:2], in0=st8[:, :, 0:2], scalar1=1.0 / 64, scalar2=None, op0=ALU.mult), reads=[K2("st8")], writes=[K2("st8")])
            tt("dve", st8[:, :, 2], st8[:, :, 0], st8[:, :, 0], ALU.mult, [K2("st8")], [K2("st8")])
            tt("dve", st8[:, :, 3], st8[:, :, 1], st8[:, :, 2], ALU.subtract, [K2("st8")], [K2("st8")])
            act(st8[:, :, 3], st8[:, :, 3], AF.Ln, [K2("st8")], [K2("st8")], bias=GN_EPS)
            act(st8[:, :, 3], st8[:, :, 3], AF.Exp, [K2("st8")], [K2("st8")], scale=-0.5)
            tt("dve", yc, pY, st8[:, :, 0:1].to_broadcast([128, 8, 64]), ALU.subtract, [("ps", bY), K2("st8"), SK], [SK])
            tt("dve", yh, yc, st8[:, :, 3:4].to_broadcast([128, 8, 64]), ALU.mult, [SK, K2("st8")], [K2("yh")])
            bYT = next_ps()
            pYT = ps[bYT][:].bitcast(BF16)
            for par in range(2):
                for j in range(8):
                    P.add("pe", lambda e, j=j, par=par: e.transpose(
                        out=pYT[par * 64:(par + 1) * 64, j * 64:(j + 1) * 64], in_=yh[par * 64:(par + 1) * 64, j, :],
                        identity=identb[par * 64:(par + 1) * 64, par * 64:(par + 1) * 64]), reads=[K2("yh"), "identb"], writes=[("ps", bYT)])
            tt("dve", z1, v3(pYT[:, 0:512]), cT[:, :, R_GNG:R_GNG + 1].to_broadcast([128, 8, 64]), ALU.mult, [("ps", bYT), "cT"], [K2("z1")])
            tt("pool", z1, z1, cT[:, :, R_GNB:R_GNB + 1].to_broadcast([128, 8, 64]), ALU.add, [K2("z1"), "cT"], [K2("z1")])
            tt("pool", z1, z1, bonT, ALU.add, [K2("z1"), QK("bonT")], [K2("z1")])
            tt("pool", zT, z1, gT, ALU.mult, [K2("z1"), QK("gT")], [K2("zT")])
            lo = 16 if first else 0
            b = proj(Wo, ("A1", "Wo"), zT, K2("zT"))
            tt("dve", hT[:, :, tc0 + lo:tc0 + n], v3(ps[b][:])[:, :, lo:n], hT[:, :, tc0 + lo:tc0 + n], ALU.add, [("ps", b), HK], [HK])

        ringkeys = []
        for nm in ("r32", "k32", "kk32", "b32"):
            ringkeys.append((K2((nm, "j", 1)), 0))
        ringkeys.append((K2(("ew", "j", 1)), 1))
        for nm in ("Vtok", "gT", "bonT"):
            ringkeys.append((K2((nm, "q", 2)), 1))
        for k_, sl_ in ringkeys:
            r = P.reg.get(("ring", sl_))
            if r is not None:
                P.reg[k_] = [r[0], r[1], list(r[2])]
        L1, L2, L3 = [], [], []
        for ti in range(NTI):
            P.defer = []
            psr["lo"], psr["hi"] = 0, 3
            gen_P1(ti)
            L1.append(P.defer)
            P.defer = []
            psr["lo"], psr["hi"] = 3, 6
            gen_P2(ti)
            L2.append(P.defer)
            P.defer = []
            psr["lo"], psr["hi"] = 6, 8
            gen_S(ti)
            L3.append(P.defer)
        P.defer = None
        psr["lo"], psr["hi"] = 0, 8

        seq_ops = []
        for ti in range(NTI):
            seq_ops += L1[ti] + L2[ti] + L3[ti]
        ms = P.schedule(seq_ops, {"pe": 0.045, "act": 0.7, "dve": 0.7, "pool": 1.15, "sp": 2.0})
        if g == 0:
            print("rwkv list schedule: est makespan us", ms)
        for sl_ in range(2):
            joined = []
            for k_, s2 in ringkeys:
                if s2 == sl_:
                    r = P.reg.get(k_)
                    if r is not None:
                        if r[0] is not None:
                            joined.append((r[0], r[1]))
                        joined.extend(r[2])
            P.reg[("ring", sl_)] = [None, None, joined]
        joined = []
        for ti in range(NTI):
            r = P.reg.pop(("hT", ti), None)
            if r is not None:
                if r[0] is not None:
                    joined.append((r[0], r[1]))
                joined.extend(r[2])
        P.reg["hT"] = [None, None, joined]
        srow2 = scrS

        if g == 0:
            AR2.reset()
            Ss = AR2.take([128, 64, 64], F32); Tm = AR2.take([128, 64, 64], F32); T2 = AR2.take([128, 64, 64], F32)
            Xt = AR2.take([128, 8, 128], F32)
            Xp = [AR2.take([128, 128], F32) for _ in range(7)]
            sa = AR2.take([128, 64], F32); ys = AR2.take([128, 128], F32); y2 = AR2.take([128, 128], F32)
            s8 = AR2.take([128, 2, 8], F32); zp = AR2.take([128, 128], F32)
            Zt = AR2.take([128, 1024], F32); zTs = AR2.take([128, 8, 16], BF16)
            for h in range(2):
                b = next_ps()
                for jj in range(4):
                    j = h * 4 + jj
                    P.add("pe", lambda e, j=j, jj=jj, b=b: e.transpose(out=ps[b][0:112, jj * 128:(jj + 1) * 128], in_=sampF[:, j, :], identity=identf[:]),
                          reads=["sampF", "identf"], writes=[("ps", b)])
                ev("act", Xt[0:112, h * 4:h * 4 + 4, :], v3(ps[b][0:112, :], 4), [("ps", b)], [K2("Xt")])
            P.add("sp", lambda e: e.dma_start(out=scr1[:, :], in_=Xt[0:112, :, :].rearrange("p j e -> p (j e)")),
                  reads=[K2("Xt")], writes=["scr1"], dsem=21)
            for m in range(7):
                P.add("sp", lambda e, m=m: e.dma_start(out=Xp[m], in_=scr1[m * 16:(m + 1) * 16, :].rearrange("t (j e) -> (t j) e", e=128)),
                      reads=["scr1"], writes=[K2(("Xp", m))], dsem=17)
            XK = [K2(("Xp", m)) for m in range(7)] + ["CGp", "CGp", "CGp"]
            last = P.reg[K2(("Xp", 6))]
            for k_ in XK[:7]:
                P.reg[k_] = [last[0], last[1], []]
            CG = [CGp[:, ci, :] for ci in range(3)]
            bcv = lambda ap: ap.unsqueeze(1).to_broadcast([128, 64, 64])
            bck = lambda ap: ap.unsqueeze(2).to_broadcast([128, 64, 64])
            for par in range(2):
                cs_ = slice(par * 64, (par + 1) * 64)
                P.add("sp", lambda e, par=par: e.dma_start(out=Ss.rearrange("p v k -> p (v k)"), in_=swkv[:, par * 4096:(par + 1) * 4096]),
                      writes=[K2("Ss")], dsem=18)
                tt("dve", Tm, Ss, bcv(Xp[0][:, cs_]), ALU.mult, [K2("Ss"), XK[0]], [K2("Tm")])
                P.add("dve", lambda e: e.tensor_reduce(out=sa, in_=Tm, axis=AX.X, op=ALU.add), reads=[K2("Tm")], writes=[K2("sa")])
                tt("pool", T2, bck(Xp[3][:, cs_]), bcv(Xp[4][:, cs_]), ALU.mult, [XK[3], XK[4]], [K2("T2")])
                tt("dve", Ss, Ss, bcv(Xp[1][:, cs_]), ALU.mult, [K2("Ss"), XK[1]], [K2("Ss")])
                tt("pool", Tm, bck(sa), bcv(Xp[2][:, cs_]), ALU.mult, [K2("sa"), XK[2], K2("Tm")], [K2("Tm")])
                tt("dve", Ss, Ss, Tm, ALU.subtract, [K2("Ss"), K2("Tm")], [K2("Ss")])
                tt("dve", Ss, Ss, T2, ALU.add, [K2("Ss"), K2("T2")], [K2("Ss")])
                P.add("sp", lambda e, par=par: e.dma_start(out=o_wkvs[:, par * 4096:(par + 1) * 4096], in_=Ss.rearrange("p v k -> p (v k)")),
                      reads=[K2("Ss")], writes=[("o_wkvs", par)], dsem=19, final=True)
                tt("dve", Tm, Ss, bcv(Xp[5][:, cs_]), ALU.mult, [K2("Ss"), XK[5], K2("Tm")], [K2("Tm")])
                P.add("dve", lambda e, cs_=cs_: e.tensor_reduce(out=ys[:, cs_], in_=Tm, axis=AX.X, op=ALU.add), reads=[K2("Tm")], writes=[K2("ys")])
            y3 = ys.rearrange("p (a c) -> p a c", a=2)
            P.add("dve", lambda e: e.tensor_reduce(out=s8[:, :, 0], in_=y3, axis=AX.X, op=ALU.add), reads=[K2("ys")], writes=[K2("s8")])
            tt("dve", y2, ys, ys, ALU.mult, [K2("ys")], [K2("y2")])
            P.add("dve", lambda e: e.tensor_reduce(out=s8[:, :, 1], in_=y2.rearrange("p (a c) -> p a c", a=2), axis=AX.X, op=ALU.add), reads=[K2("y2"), K2("s8")], writes=[K2("s8")])
            P.add("dve", lambda e: e.tensor_scalar(out=s8[:, :, 0:2], in0=s8[:, :, 0:2], scalar1=1.0 / 64, scalar2=None, op0=ALU.mult), reads=[K2("s8")], writes=[K2("s8")])
            tt("dve", s8[:, :, 2], s8[:, :, 0], s8[:, :, 0], ALU.mult, [K2("s8")], [K2("s8")])
            tt("dve", s8[:, :, 3], s8[:, :, 1], s8[:, :, 2], ALU.subtract, [K2("s8")], [K2("s8")])
            act(s8[:, :, 3], s8[:, :, 3], AF.Sqrt, [K2("s8")], [K2("s8")], bias=GN_EPS)
            P.add("dve", lambda e: e.reciprocal(out=s8[:, :, 3], in_=s8[:, :, 3]), reads=[K2("s8")], writes=[K2("s8")])
            z3 = zp.rearrange("p (a c) -> p a c", a=2)
            tt("dve", z3, y3, s8[:, :, 0:1].to_broadcast([128, 2, 64]), ALU.subtract, [K2("ys"), K2("s8")], [K2("zp")])
            tt("dve", z3, z3, s8[:, :, 3:4].to_broadcast([128, 2, 64]), ALU.mult, [K2("zp"), K2("s8")], [K2("zp")])
            tt("dve", zp, zp, CG[0], ALU.mult, [K2("zp"), XK[7]], [K2("zp")])
            tt("dve", zp, zp, CG[1], ALU.add, [K2("zp"), XK[8]], [K2("zp")])
            tt("dve", y2, Xp[5], Xp[4], ALU.mult, [XK[5], XK[4], K2("y2")], [K2("y2")])
            tt("dve", y2, y2, CG[2], ALU.mult, [K2("y2"), XK[9]], [K2("y2")])
            P.add("dve", lambda e: e.tensor_reduce(out=s8[:, :, 4], in_=y2.rearrange("p (a c) -> p a c", a=2), axis=AX.X, op=ALU.add), reads=[K2("y2"), K2("s8")], writes=[K2("s8")])
            tt("dve", y2.rearrange("p (a c) -> p a c", a=2), Xp[3].rearrange("p (a c) -> p a c", a=2), s8[:, :, 4:5].to_broadcast([128, 2, 64]), ALU.mult,
               [XK[3], K2("s8"), K2("y2")], [K2("y2")])
            tt("dve", zp, zp, y2, ALU.add, [K2("zp"), K2("y2")], [K2("zp")])
            tt("dve", zp, zp, Xp[6], ALU.mult, [K2("zp"), XK[6]], [K2("zp")])
            P.add("sp", lambda e: e.dma_start(out=scr2[:, :], in_=zp), reads=[K2("zp")], writes=["scr2"], dsem=20)
            P.add("sp", lambda e: e.dma_start(out=Zt[0:16, :], in_=scr2.rearrange("(t j) e -> t (j e)", j=8)), reads=["scr2"], writes=[K2("Zt")], dsem=22)
            b = next_ps()
            for j in range(8):
                P.add("pe", lambda e, j=j, b=b: e.transpose(out=ps[b][:, j * 16:(j + 1) * 16], in_=Zt[0:16, j * 128:(j + 1) * 128], identity=identf[0:16, 0:16]),
                      reads=[K2("Zt"), "identf"], writes=[("ps", b)])
            ev("act", zTs, v3(ps[b][:, 0:128]), [("ps", b)], [K2("zTs")])
            b = next_ps()
            for o in range(8):
                for k in range(8):
                    P.add("pe", lambda e, o=o, k=k, b=b: e.matmul(ps[b][:, o * 16:(o + 1) * 16], lhsT=Wo[:, k, o * 128:(o + 1) * 128], rhs=zTs[:, k, :],
                                                                  start=(k == 0), stop=(k == 7)), reads=[("A1", "Wo"), K2("zTs")], writes=[("ps", b)])
            tt("dve", hT[:, :, 0:16], v3(ps[b][:, 0:128]), hT[:, :, 0:16], ALU.add, [("ps", b), "hT"], ["hT"])
        if g == NGRP - 1:
            for h in range(2):
                b = next_ps()
                for jj in range(4):
                    j = h * 4 + jj
                    P.add("pe", lambda e, j=j, jj=jj, b=b: e.transpose(out=ps[b][0:64, jj * 128:(jj + 1) * 128], in_=Sst[:, j, :], identity=identf[:]),
                          reads=["Sst", "identf"], writes=[("ps", b)])
                ev("act", scrS[0:64, h * 512:(h + 1) * 512], ps[b][0:64, :], [("ps", b)], [K2("scrS")])
            P.add("sp", lambda e: e.dma_start(out=o_wkvp.rearrange("h v k -> v h k"), in_=scrS[0:64, :].rearrange("p (h k) -> p h k", k=64)),
                  reads=[K2("scrS")], writes=["o_wkvp"], dsem=14, final=True)

    for g in range(NGRP):
        c0 = g * NG
        AR2.reset()
        xrow = [AR2.take([128, 1024], F32) for _ in range(2)]
        rows = []
        if g == 0:
            rows.append(("first", 64, 0))
            for i in range(5):
                rows.append((i * 128, 128, 64 + i * 128))
        else:
            r0 = c0 - 64
            for (o, m) in tiles_of(NG, 128):
                rows.append((r0 + o, m, o))
        for ti, (src, nr, dc) in enumerate(rows):
            xb = xrow[ti % 2]
            key = ("A2", "xrow", ti % 2)
            if src == "first":
                P.add("dve", lambda e, xb=xb: e.memset(xb[0:64, :], 0.0), writes=[key])
                P.add("sp", lambda e, xb=xb: e.dma_start(out=xb[0:16, :], in_=xs[:, :]), reads=[key], writes=[("A2", "xf0")], dsem=9)
                P.add("sp", lambda e, xb=xb: e.dma_start(out=xb[16:32, :], in_=sshift[:, :]), reads=[key], writes=[("A2", "xf1")], dsem=9)
                P.add("sp", lambda e, xb=xb: e.dma_start(out=xb[48:64, :], in_=meta[:, :]), reads=[key], writes=[("A2", "xf2")], dsem=9)
                r_ = P.reg[("A2", "xf2")]
                P.reg[key] = [r_[0], r_[1], []]
            else:
                P.add("sp", lambda e, xb=xb, src=src, nr=nr: e.dma_start(out=xb[0:nr, :], in_=xp[src:src + nr, :]), writes=[key], dsem=10 + ti % 2)
            for h in range(2):
                b = next_ps()
                for kk_ in range(4):
                    k = h * 4 + kk_
                    P.add("pe", lambda e, xb=xb, nr=nr, k=k, kk_=kk_, b=b: e.transpose(
                        out=ps[b][:, kk_ * 128:kk_ * 128 + nr], in_=xb[0:nr, k * 128:(k + 1) * 128], identity=identf[0:nr, 0:nr]),
                        reads=[key, "identf"], writes=[("ps", b)])
                P.add("act" if h == 0 else "dve",
                      (lambda e, h=h, b=b, nr=nr, dc=dc: e.activation(out=hT[:, h * 4:h * 4 + 4, dc:dc + nr],
                                                                      in_=ps[b][:].rearrange("p (k n) -> p k n", k=4)[:, :, 0:nr], func=AF.Copy))
                      if h == 0 else
                      (lambda e, h=h, b=b, nr=nr, dc=dc: e.tensor_copy(out=hT[:, h * 4:h * 4 + 4, dc:dc + nr],
                                                                       in_=ps[b][:].rearrange("p (k n) -> p k n", k=4)[:, :, 0:nr])),
                      reads=[("ps", b)], writes=["hT"])
        if g == 0:
            P.add("dve", lambda e: e.tensor_copy(out=shiftT[:], in_=hT[:, :, 16:32]), reads=["hT"], writes=["shiftT"])

        if stage >= 2:
            AR2.reset()
            AR1.reset()
            P.defer = []
            hnb = AR2.take([128, 8, NG], BF16)
            cbuf = AR2.take([128, 8, NG], F32)
            gluj = [AR2.take([128, 30 + NG], BF16) for _ in range(2)]
            dg = [AR2.take([128, 31, 128], BF16) for _ in range(2)]
            sg = [AR2.take([128, 352], F32) for _ in range(2)]
            rmsnorm(hT, NG, R_NMIX + 0, hnb, AR2, "A2", tilekeys=True)
            HN = lambda k, t0: ("A2", "dst", k, t0)
            for j in range(8):
                if j % 2 == 0:
                    q = j // 2
                    slot = ring_fill([
                        lambda r, q=q: (r[:].rearrange("p (k n) -> p k n", k=8)[:, :, 0:256], wv(w_pw1)[:, :, q * 256:(q + 1) * 256]),
                        lambda r, q=q: (r[:].rearrange("p (k n) -> p k n", k=8)[:, :, 256:512], wv(w_pw1)[:, :, D + q * 256:D + (q + 1) * 256])])
                W = ring[slot][:].rearrange("p (k n) -> p k n", k=8)
                gj = gluj[j % 2]
                gk = ("A2", "gluj", j % 2)
                P.add("pool", lambda e, gj=gj, j=j: e.tensor_copy(out=gj[:, 0:30], in_=gtail[:, j, :]), reads=["gtail"], writes=[gk])
                for ti, (t0, m) in enumerate(TL):
                    ba, bb = next_ps(), next_ps()
                    for k in range(8):
                        P.add("pe", lambda e, k=k, W=W, j=j, t0=t0, m=m, ba=ba: e.matmul(
                            ps[ba][:, 0:m], lhsT=W[:, k, (j % 2) * 128:(j % 2) * 128 + 128], rhs=hnb[:, k, t0:t0 + m], start=(k == 0), stop=(k == 7)),
                            reads=[("ring", slot), HN(k, t0)], writes=[("ps", ba)])
                    for k in range(8):
                        P.add("pe", lambda e, k=k, W=W, j=j, t0=t0, m=m, bb=bb: e.matmul(
                            ps[bb][:, 0:m], lhsT=W[:, k, 256 + (j % 2) * 128:256 + (j % 2) * 128 + 128], rhs=hnb[:, k, t0:t0 + m], start=(k == 0), stop=(k == 7)),
                            reads=[("ring", slot), ("ringg", slot), HN(k, t0)], writes=[("ps", bb)])
                    sgt = sg[ti % 2]
                    P.add("act", lambda e, sgt=sgt, bb=bb, m=m, j=j: e.activation(out=sgt[:, 0:m], in_=ps[bb][:, 0:m], func=AF.Sigmoid, bias=cv(R_BPW1 + 1, j)),
                          reads=[("ps", bb), "cT"], writes=[("A2", "sg", ti % 2)])
                    P.add("dve", lambda e, sgt=sgt, ba=ba, m=m, j=j, gj=gj, t0=t0: e.scalar_tensor_tensor(
                        out=gj[:, 30 + t0:30 + t0 + m], in0=ps[ba][:, 0:m], scalar=cv(R_BPW1, j), in1=sgt[:, 0:m], op0=ALU.add, op1=ALU.mult),
                        reads=[("ps", ba), ("A2", "sg", ti % 2), "cT"], writes=[gk])
                    if g == 0 and ti == 0:
                        P.add("dve", lambda e, sgt=sgt, ba=ba, j=j: e.scalar_tensor_tensor(
                            out=g32s[:, j, :], in0=ps[ba][:, 0:16], scalar=cv(R_BPW1, j), in1=sgt[:, 0:16], op0=ALU.add, op1=ALU.mult),
                            reads=[("ps", ba), ("A2", "sg", ti % 2), "cT"], writes=["g32s"])
                    if g == NGRP - 1 and ti == len(TL) - 1:
                        P.add("dve", lambda e, sgt=sgt, ba=ba, j=j, m=m: e.scalar_tensor_tensor(
                            out=g32p[:, j, :], in0=ps[ba][:, m - 30:m], scalar=cv(R_BPW1, j), in1=sgt[:, m - 30:m], op0=ALU.add, op1=ALU.mult),
                            reads=[("ps", ba), ("A2", "sg", ti % 2), "cT"], writes=["g32p"])
                if g == 0:
                    P.add("dve", lambda e, gj=gj: e.memset(gj[:, 30 + 16:30 + 48], 0.0), reads=[gk], writes=[gk])
                P.add("pool", lambda e, gj=gj, j=j: e.tensor_copy(out=gtail[:, j, :], in_=gj[:, NG:NG + 30]), reads=[gk], writes=["gtail"])
                dj = dg[j % 2]
                dk = ("A2", "dg", j % 2)
                P.add("dve", lambda e, dj=dj, j=j: e.tensor_tensor(
                    out=dj, in0=identb[:].unsqueeze(1).to_broadcast([128, 31, 128]),
                    in1=cT[:, j, R_WDW:R_WDW + 31].unsqueeze(2).to_broadcast([128, 31, 128]), op=ALU.mult),
                    reads=["identb", "cT"], writes=[dk], dur=4.5)
                for (t0, m) in TL:
                    b = next_ps()
                    for tap in range(31):
                        P.add("pe", lambda e, dj=dj, gj=gj, tap=tap, t0=t0, m=m, b=b: e.matmul(
                            ps[b][:, 0:m], lhsT=dj[:, tap, :], rhs=gj[:, t0 + tap:t0 + tap + m], start=(tap == 0), stop=(tap == 30)),
                            reads=[dk, gk], writes=[("ps", b)])
                    P.add("act", lambda e, b=b, m=m, t0=t0, j=j: e.activation(out=cbuf[:, j, t0:t0 + m], in_=ps[b][:, 0:m], func=AF.Identity, bias=cv(R_BDW, j)),
                          reads=[("ps", b), "cT"], writes=[("A2", "c", j, t0)])
            CKf = lambda k, t0: ("A2", "c", k, t0)
            CK = [CKf(k, 0) for k in range(8)]
            if g == 0:
                scT = AR1.take([128, 8, 480], F32)
                sctmp = AR1.take([128, 8, 480], F32)
                srow = [AR1.take([128, 1024], F32) for _ in range(2)]
                red = AR1.take([128, 8, 16], F32)
                for i in range(4):
                    nr = 128 if i < 3 else 96
                    sk = ("A1", "srow", i % 2)
                    P.add("sp", lambda e, i=i, nr=nr: e.dma_start(out=srow[i % 2][0:nr, :], in_=sconv[i * 128:i * 128 + nr, :]), writes=[sk], dsem=12 + i % 2)
                    for h in range(2):
                        b = next_ps()
                        for kk_ in range(4):
                            k = h * 4 + kk_
                            P.add("pe", lambda e, i=i, nr=nr, k=k, kk_=kk_, b=b: e.transpose(
                                out=ps[b][:, kk_ * 128:kk_ * 128 + nr], in_=srow[i % 2][0:nr, k * 128:(k + 1) * 128], identity=identf[0:nr, 0:nr]),
                                reads=[sk, "identf"], writes=[("ps", b)])
                        P.add("act", lambda e, h=h, b=b, nr=nr, i=i: e.activation(
                            out=scT[:, h * 4:h * 4 + 4, i * 128:i * 128 + nr], in_=ps[b][:].rearrange("p (k n) -> p k n", k=4)[:, :, 0:nr], func=AF.Copy),
                            reads=[("ps", b)], writes=[("A1", "scT")])
                P.add("sp", lambda e: e.dma_start(out=o_convs.rearrange("(t r) d -> t r d", r=30)[:, 0:29, :],
                                                  in_=sconv.rearrange("(t r) d -> t r d", r=30)[:, 1:30, :]), writes=["o_convs_a"], dsem=14, final=True)
                for k in range(8):
                    P.add("dve", lambda e, k=k: e.tensor_tensor(
                        out=sctmp[:, k, :].rearrange("p (t r) -> p t r", r=30), in0=scT[:, k, :].rearrange("p (t r) -> p t r", r=30),
                        in1=cT[:, k, R_WDW:R_WDW + 30].unsqueeze(1).to_broadcast([128, 16, 30]), op=ALU.mult),
                        reads=[("A1", "scT"), "cT"], writes=[("A1", "sctmp")])
                P.add("dve", lambda e: e.tensor_reduce(out=red, in_=sctmp.rearrange("p k (t r) -> p k t r", r=30), axis=AX.X, op=ALU.add),
                      reads=[("A1", "sctmp")], writes=[("A1", "red")])
                for k in range(8):
                    P.add("dve", lambda e, k=k: e.scalar_tensor_tensor(out=red[:, k, :], in0=g32s[:, k, :], scalar=cv(R_WDW + 30, k), in1=red[:, k, :],
                                                                       op0=ALU.mult, op1=ALU.add), reads=["g32s", ("A1", "red"), "cT"], writes=[("A1", "red")])
                    P.add("dve", lambda e, k=k: e.tensor_scalar(out=cbuf[:, k, 0:16], in0=red[:, k, :], scalar1=cv(R_BDW, k), scalar2=None, op0=ALU.add),
                          reads=[("A1", "red"), "cT"], writes=[CK[k]])
                for h in range(2):
                    b = next_ps()
                    for kk_ in range(4):
                        k = h * 4 + kk_
                        P.add("pe", lambda e, k=k, kk_=kk_, b=b: e.transpose(out=ps[b][0:16, kk_ * 128:(kk_ + 1) * 128], in_=g32s[:, k, :], identity=identf[:]),
                              reads=["g32s", "identf"], writes=[("ps", b)])
                    P.add("act", lambda e, h=h, b=b: e.activation(out=srow[0][0:16, h * 512:(h + 1) * 512], in_=ps[b][0:16, :], func=AF.Copy),
                          reads=[("ps", b)], writes=[("A1", "srow", 0)])
                P.add("sp", lambda e: e.dma_start(out=o_convs.rearrange("(t r) d -> t r d", r=30)[:, 29, :], in_=srow[0][0:16, :]),
                      reads=[("A1", "srow", 0)], writes=["o_convs_b"], dsem=14, final=True)
            if g == NGRP - 1:
                prow = AR1.take([128, 1024], F32)
                for h in range(2):
                    b = next_ps()
                    for kk_ in range(4):
                        k = h * 4 + kk_
                        P.add("pe", lambda e, k=k, kk_=kk_, b=b: e.transpose(out=ps[b][0:30, kk_ * 128:(kk_ + 1) * 128], in_=g32p[:, k, :], identity=identf[:]),
                              reads=["g32p", "identf"], writes=[("ps", b)])
                    P.add("act", lambda e, h=h, b=b: e.activation(out=prow[0:30, h * 512:(h + 1) * 512], in_=ps[b][0:30, :], func=AF.Copy),
                          reads=[("ps", b)], writes=[("A1", "prow")])
                P.add("sp", lambda e: e.dma_start(out=o_convp[:, :], in_=prow[0:30, :]), reads=[("A1", "prow")], writes=["o_convp"], dsem=14, final=True)
            sqb = [AR2.take([128, 352], BF16) for _ in range(2)]
            cbb = [AR2.take([128, 352], BF16) for _ in range(2)]
            mean = AR2.take([128, 352], F32)
            rstd = AR2.take([128, 352], F32)
            t1 = [AR2.take([128, 352], F32) for _ in range(2)]
            for (t0, m) in TL:
                b1, b2 = next_ps(), next_ps()
                for k in range(8):
                    P.add("act", lambda e, k=k, t0=t0, m=m: e.activation(out=sqb[k % 2][:, 0:m], in_=cbuf[:, k, t0:t0 + m], func=AF.Square),
                          reads=[CKf(k, t0)], writes=[("A2", "sqb", k % 2)])
                    P.add("dve", lambda e, k=k, t0=t0, m=m: e.tensor_copy(out=cbb[k % 2][:, 0:m], in_=cbuf[:, k, t0:t0 + m]),
                          reads=[CKf(k, t0)], writes=[("A2", "cbb", k % 2)])
                    P.add("pe", lambda e, k=k, m=m, b1=b1: e.matmul(ps[b1][:, 0:m], lhsT=onesb[:], rhs=cbb[k % 2][:, 0:m], start=(k == 0), stop=(k == 7)),
                          reads=[("A2", "cbb", k % 2), "onesb"], writes=[("ps", b1)])
                    P.add("pe", lambda e, k=k, m=m, b2=b2: e.matmul(ps[b2][:, 0:m], lhsT=onesb[:], rhs=sqb[k % 2][:, 0:m], start=(k == 0), stop=(k == 7)),
                          reads=[("A2", "sqb", k % 2), "onesb"], writes=[("ps", b2)])
                P.add("act", lambda e, m=m, b1=b1: e.activation(out=mean[:, 0:m], in_=ps[b1][:, 0:m], func=AF.Copy, scale=1.0 / D),
                      reads=[("ps", b1)], writes=[("A2", "mean")])
                P.add("dve", lambda e, m=m: e.tensor_tensor(out=rstd[:, 0:m], in0=mean[:, 0:m], in1=mean[:, 0:m], op=ALU.mult),
                      reads=[("A2", "mean")], writes=[("A2", "rstd")])
                P.add("dve", lambda e, m=m, b2=b2: e.scalar_tensor_tensor(out=rstd[:, 0:m], in0=ps[b2][:, 0:m], scalar=1.0 / D, in1=rstd[:, 0:m],
                                                                          op0=ALU.mult, op1=ALU.subtract), reads=[("ps", b2), ("A2", "rstd")], writes=[("A2", "rstd")])
                P.add("act", lambda e, m=m: e.activation(out=rstd[:, 0:m], in_=rstd[:, 0:m], func=AF.Sqrt, bias=LN_EPS), reads=[("A2", "rstd")], writes=[("A2", "rstd")])
                P.add("dve", lambda e, m=m: e.reciprocal(out=rstd[:, 0:m], in_=rstd[:, 0:m]), reads=[("A2", "rstd")], writes=[("A2", "rstd")])
                for k in range(8):
                    tt = t1[k % 2]
                    tk = ("A2", "t1", k % 2)
                    P.add("dve", lambda e, k=k, t0=t0, m=m, tt=tt: e.tensor_tensor(out=tt[:, 0:m], in0=cbuf[:, k, t0:t0 + m], in1=mean[:, 0:m], op=ALU.subtract),
                          reads=[CKf(k, t0), ("A2", "mean")], writes=[tk])
                    P.add("dve", lambda e, m=m, tt=tt: e.tensor_tensor(out=tt[:, 0:m], in0=tt[:, 0:m], in1=rstd[:, 0:m], op=ALU.mult),
                          reads=[tk, ("A2", "rstd")], writes=[tk])
                    P.add("act", lambda e, k=k, t0=t0, m=m, tt=tt: e.activation(out=hnb[:, k, t0:t0 + m], in_=tt[:, 0:m], func=AF.Silu,
                                                                                scale=cv(R_LNG, k), bias=cv(R_LNB, k)),
                          reads=[tk, "cT"], writes=[HN(k, t0)])
            for o in range(8):
                if o % 4 == 0:
                    hh = o // 4
                    slot = ring_fill([lambda r, hh=hh: (r[:].rearrange("p (k n) -> p k n", k=8), wv(w_pw2)[:, :, hh * 512:(hh + 1) * 512])])
                W = ring[slot][:].rearrange("p (k n) -> p k n", k=8)
                for (t0, m) in TL:
                    b = next_ps()
                    for k in range(8):
                        P.add("pe", lambda e, k=k, W=W, o=o, t0=t0, m=m, b=b: e.matmul(
                            ps[b][:, 0:m], lhsT=W[:, k, (o % 4) * 128:(o % 4) * 128 + 128], rhs=hnb[:, k, t0:t0 + m], start=(k == 0), stop=(k == 7)),
                            reads=[("ring", slot), HN(k, t0)], writes=[("ps", b)])
                    P.add("dve", lambda e, o=o, t0=t0, m=m, b=b: e.scalar_tensor_tensor(
                        out=hT[:, o, t0:t0 + m], in0=ps[b][:, 0:m], scalar=cv(R_BPW2, o), in1=hT[:, o, t0:t0 + m], op0=ALU.add, op1=ALU.add),
                        reads=[("ps", b), "hT", "cT"], writes=["hT"])

            opsB = P.defer
            P.defer = None
            msB = P.schedule(opsB, {"pe": 0.16, "act": 0.5, "dve": 0.5, "pool": 0.9, "sp": 2.0})
            if g == 0:
                print("phase B list schedule: est makespan us", msB)

        def mlp(l):
            AR2.reset()
            AR1.reset()
            hnb = AR2.take([128, 8, NG], BF16)
            hid = AR2.take([128, 32, NG], BF16)
            rl = [AR2.take([128, 352], F32) for _ in range(2)]
            wo = AR1.take([128, 32, 1024], BF16)
            rmsnorm(hT, NG, R_NMLP + l, hnb, AR2, "A2", tilekeys=True)
            HN = lambda k, t0: ("A2", "dst", k, t0)
            WOK = [("A1", "wo", q) for q in range(8)]
            for f in range(32):
                if f % 4 == 0:
                    q = f // 4
                    slot = ring_fill([lambda r, q=q: (r[:].rearrange("p (k n) -> p k n", k=8), wv(w_in[l])[:, :, q * 512:(q + 1) * 512])])
                    P.add("pool", lambda e, q=q: e.dma_start(out=wo[:, 4 * q:4 * q + 4, :], in_=w_out[l].rearrange("(f p) n -> p f n", p=128)[:, 4 * q:4 * q + 4, :]),
                          writes=[("A1", "wo", q)], dsem=40 + q)
                W = ring[slot][:].rearrange("p (k n) -> p k n", k=8)
                for ti, (t0, m) in enumerate(TL):
                    b = next_ps()
                    for k in range(8):
                        P.add("pe", lambda e, k=k, W=W, f=f, t0=t0, m=m, b=b: e.matmul(
                            ps[b][:, 0:m], lhsT=W[:, k, (f % 4) * 128:(f % 4) * 128 + 128], rhs=hnb[:, k, t0:t0 + m], start=(k == 0), stop=(k == 7)),
                            reads=[("ring", slot), HN(k, t0)], writes=[("ps", b)])
                    r_ = rl[ti % 2]
                    P.add("act", lambda e, r_=r_, b=b, m=m: e.activation(out=r_[:, 0:m], in_=ps[b][:, 0:m], func=AF.Relu),
                          reads=[("ps", b)], writes=[("A2", "rl", ti % 2)])
                    P.add("dve", lambda e, r_=r_, b=b, m=m, f=f, t0=t0: e.tensor_tensor(out=hid[:, f, t0:t0 + m], in0=ps[b][:, 0:m], in1=r_[:, 0:m], op=ALU.mult),
                          reads=[("ps", b), ("A2", "rl", ti % 2)], writes=[("A2", "hid", f)])
            for o in range(8):
                for (t0, m) in TL:
                    b = next_ps()
                    for f in range(32):
                        P.add("pe", lambda e, f=f, o=o, t0=t0, m=m, b=b: e.matmul(
                            ps[b][:, 0:m], lhsT=wo[:, f, o * 128:(o + 1) * 128], rhs=hid[:, f, t0:t0 + m], start=(f == 0), stop=(f == 31)),
                            reads=[WOK[f // 4], ("A2", "hid", f)], writes=[("ps", b)])
                    P.add("dve", lambda e, o=o, t0=t0, m=m, b=b: e.tensor_tensor(out=hT[:, o, t0:t0 + m], in0=ps[b][:, 0:m], in1=hT[:, o, t0:t0 + m], op=ALU.add),
                          reads=[("ps", b), "hT"], writes=["hT"])

        if stage >= 3:
            mlp(0)
        if stage >= 4:
            rwkv(g)
        if stage >= 5:
            mlp(1)

        AR2.reset()
        yfin = AR2.take([128, 8, NG], F32)
        yrow = [AR2.take([128, 1024], F32) for _ in range(2)]
        rmsnorm(hT, NG, R_NFIN, yfin, AR2, "A2", tilekeys=True)
        YKt = lambda k, c0_, m_: [("A2", "dst", k, t0) for (t0, mm) in TL if t0 < c0_ + m_ and c0_ < t0 + mm]
        outs = []
        if g == 0:
            outs.append((0, 16, y_s[:, :]))
            for i in range(5):
                outs.append((64 + i * 128, 128, y_p[i * 128:(i + 1) * 128, :]))
        else:
            r0 = c0 - 64
            for (o, m) in tiles_of(NG, 128):
                outs.append((o, m, y_p[r0 + o:r0 + o + m, :]))
        for oi, (col, m, dst) in enumerate(outs):
            yb = yrow[oi % 2]
            yk = ("A2", "yrow", oi % 2)
            for h in range(2):
                b = next_ps()
                for kk_ in range(4):
                    k = h * 4 + kk_
                    P.add("pe", lambda e, k=k, kk_=kk_, b=b, col=col, m=m: e.transpose(out=ps[b][0:m, kk_ * 128:(kk_ + 1) * 128], in_=yfin[:, k, col:col + m], identity=identf[:]),
                          reads=YKt(k, col, m) + ["identf"], writes=[("ps", b)])
                if h == 0:
                    P.add("act", lambda e, yb=yb, b=b, m=m: e.activation(out=yb[0:m, 0:512], in_=ps[b][0:m, :], func=AF.Copy), reads=[("ps", b)], writes=[yk])
                else:
                    P.add("dve", lambda e, yb=yb, b=b, m=m: e.tensor_copy(out=yb[0:m, 512:1024], in_=ps[b][0:m, :]), reads=[("ps", b)], writes=[yk])
            P.add("sp", lambda e, yb=yb, m=m, dst=dst: e.dma_start(out=dst, in_=yb[0:m, :]), reads=[yk], writes=[("y", g, oi)], dsem=15 + oi % 2, final=True)

    P.emit()
    st.close()
    return nc


_CACHE = {}


def _prep_inputs(inp):
    f = lambda a: np.ascontiguousarray(np.asarray(a, dtype=np.float32))
    vec_rows = [inp["norm_mix"][0], inp["norm_mix"][1], inp["norm_mlp"][0], inp["norm_mlp"][1], inp["norm_final"],
                inp["conv_b_pw1"][0][:D], inp["conv_b_pw1"][0][D:]]
    vec_rows += [inp["conv_w_dw"][0][j] for j in range(31)]
    vec_rows += [inp["conv_b_dw"][0], inp["conv_ln_g"][0], inp["conv_ln_b"][0], inp["conv_b_pw2"][0]]
    vec_rows += [inp["rwkv_x_mix"][0][m] for m in range(6)]
    vec_rows += [inp["rwkv_w0"][0], inp["rwkv_a0"][0], inp["rwkv_k_k"][0], inp["rwkv_k_a"][0],
                 np.asarray(inp["rwkv_r_k"][0]).reshape(-1), inp["rwkv_gn_g"][0], inp["rwkv_gn_b"][0]]
    vecs = f(np.stack([np.asarray(v, dtype=np.float32) for v in vec_rows], 0))
    shared = {
        "meta": f(inp["meta_tokens"]), "vecs": vecs,
        "w_pw1": f(inp["conv_w_pw1"][0]), "w_pw2": f(inp["conv_w_pw2"][0]),
        "w_r": f(inp["rwkv_w_r"][0]), "w_k": f(inp["rwkv_w_k"][0]), "w_v": f(inp["rwkv_w_v"][0]), "w_o": f(inp["rwkv_w_o"][0]),
        "w1": f(inp["rwkv_w1"][0]), "w2": f(inp["rwkv_w2"][0]), "a1": f(inp["rwkv_a1"][0]), "a2": f(inp["rwkv_a2"][0]),
        "g1": f(inp["rwkv_g1"][0]), "g2": f(inp["rwkv_g2"][0]),
        "w_in": f(inp["w_mlp_in"]), "w_out": f(inp["w_mlp_out"]),
    }
    maps = []
    for c in range(8):
        m = dict(shared)
        sl = slice(16 * c, 16 * c + 16)
        m["xp"] = f(inp["x_prompt"][c])
        m["xs"] = f(np.asarray(inp["x_sample"])[sl, 0, :])
        m["sconv"] = f(np.asarray(inp["state_conv"])[0, sl].reshape(480, D))
        m["sshift"] = f(np.asarray(inp["state_shift"])[0, sl])
        m["swkv"] = f(np.asarray(inp["state_wkv"])[0, sl].reshape(128, 8192))
        maps.append(m)
    return maps


def kernel(**inputs):
    stage = inputs.pop("_stage", 9)
    if stage not in _CACHE:
        _CACHE[stage] = build(stage)
    nc = _CACHE[stage]
    maps = _prep_inputs(inputs)
    res = run_bass_kernel_spmd(nc, maps, core_ids=list(range(8)))
    R = res.results
    y_prompt = np.stack([R[c]["y_p"] for c in range(8)], 0)
    y_sample = np.concatenate([R[c]["y_s"] for c in range(8)], 0).reshape(128, 1, D)
    conv_prompt = np.stack([R[c]["o_convp"] for c in range(8)], 0)[None]
    shift_prompt = np.stack([R[c]["o_shiftp"].reshape(D) for c in range(8)], 0)[None]
    wkv_prompt = np.stack([R[c]["o_wkvp"] for c in range(8)], 0)[None]
    conv_sample = np.concatenate([R[c]["o_convs"].reshape(16, 30, D) for c in range(8)], 0)[None]
    shift_sample = np.concatenate([R[c]["o_shifts"] for c in range(8)], 0)[None]
    wkv_sample = np.concatenate([R[c]["o_wkvs"].reshape(16, 16, 64, 64) for c in range(8)], 0)[None]
    return tuple(np.ascontiguousarray(a, dtype=np.float32) for a in
                 (y_prompt, y_sample, conv_prompt, shift_prompt, wkv_prompt, conv_sample, shift_sample, wkv_sample))
```

```python
import numpy as np
from contextlib import ExitStack
import concourse.bass as bass
import concourse.mybir as mybir
from concourse.bass_utils import run_bass_kernel_spmd

F32 = mybir.dt.float32
BF16 = mybir.dt.bfloat16
AF = mybir.ActivationFunctionType
ALU = mybir.AluOpType
AX = mybir.AxisListType

ENGS = ("pe", "act", "dve", "pool", "sp")
D = 1024
NT = 2112
NG = 704
NGRP = 3
RMS_EPS = 1e-6
LN_EPS = 1e-5
GN_EPS = 64e-5


class Prog:
    def __init__(self, nc):
        self.nc = nc
        self.ops = {e: [] for e in ENGS}
        self.cnt = {e: 0 for e in ENGS}
        self.clock = {e: {} for e in ENGS}
        self.reg = {}
        self.dma_cnt = {}
        self.final = []
        self.pending = {}
        self.defer = None
        self.swdge_hist = []

    def _get(self, k, create=False):
        r = self.reg.get(k)
        if r is None and isinstance(k, tuple) and k[0] in self.pending:
            r = [None, None, list(self.pending[k[0]])]
            self.reg[k] = r
        if r is None and create:
            r = [None, None, []]
            self.reg[k] = r
        return r

    def fence(self, prefix):
        deps = list(self.pending.get(prefix, []))
        for k in list(self.reg.keys()):
            if isinstance(k, tuple) and k[0] == prefix:
                r = self.reg.pop(k)
                if r[0] is not None:
                    deps.append((r[0], r[1]))
                deps.extend(r[2])
        best = {}
        for tok, clk in deps:
            if tok[0] not in best or best[tok[0]][0][1] < tok[1]:
                best[tok[0]] = (tok, clk)
        self.pending[prefix] = list(best.values())

    def schedule(self, flat, DUR):
        import heapq
        units = []
        op2unit = []
        i = 0
        while i < len(flat):
            j = i + 1
            if flat[i][0] == "pe":
                while j < len(flat) and flat[j][0] == "pe" and j - i < 64:
                    j += 1
            d = 0.0
            for op in flat[i:j]:
                d += op[6] if (len(op) > 6 and op[6] is not None) else DUR[op[0]]
            units.append((flat[i][0], flat[i:j], d))
            op2unit += [len(units) - 1] * (j - i)
            i = j
        nU = len(units)
        udeps = [set() for _ in range(nU)]
        reg_ = {}
        for idx, op in enumerate(flat):
            eng, reads, writes = op[0], op[2], op[3]
            u = op2unit[idx]
            for k in reads:
                r = reg_.get(k)
                if r is not None and r[0] is not None and r[0] != u:
                    udeps[u].add(r[0])
                if r is not None and isinstance(k, tuple) and k[0] == "ps":
                    for x in r[1]:
                        if x != u and units[x][0] != eng:
                            udeps[u].add(x)
            for k in writes:
                r = reg_.get(k)
                if r is not None:
                    if r[0] is not None and r[0] != u:
                        udeps[u].add(r[0])
                    for x in r[1]:
                        if x != u:
                            udeps[u].add(x)
            for k in reads:
                reg_.setdefault(k, [None, []])[1].append(u)
            for k in writes:
                reg_[k] = [u, []]
        succ = [[] for _ in range(nU)]
        for u in range(nU):
            for d in udeps[u]:
                succ[d].append(u)
        cp = [0.0] * nU
        for u in range(nU - 1, -1, -1):
            m = 0.0
            for v in succ[u]:
                if cp[v] > m:
                    m = cp[v]
            cp[u] = m + units[u][2] + 0.6
        ndep = [len(udeps[u]) for u in range(nU)]
        ready_t = [0.0] * nU
        heaps = {e: [] for e in ENGS}
        for u in range(nU):
            if ndep[u] == 0:
                heapq.heappush(heaps[units[u][0]], (0.0, -cp[u], u))
        efree = {e: 0.0 for e in ENGS}
        order = []
        done = 0
        while done < nU:
            best = None
            for e in ENGS:
                h = heaps[e]
                if not h:
                    continue
                tfree = efree[e]
                cands = []
                while h and h[0][0] <= tfree:
                    cands.append(heapq.heappop(h))
                if cands:
                    cands.sort(key=lambda x: (x[1], x[2]))
                    pick = cands[0]
                    for c in cands[1:]:
                        heapq.heappush(h, c)
                    start = tfree
                else:
                    pick = heapq.heappop(h)
                    start = pick[0]
                if best is None or start < best[0]:
                    if best is not None:
                        heapq.heappush(heaps[best[2]], best[1])
                    best = (start, pick, e)
                else:
                    heapq.heappush(h, pick)
            start, pick, e = best
            u = pick[2]
            fin = start + units[u][2]
            efree[e] = fin
            order.append((start, u))
            done += 1
            for v in succ[u]:
                lat = 0.1 if units[v][0] == e else 0.6
                if fin + lat > ready_t[v]:
                    ready_t[v] = fin + lat
                ndep[v] -= 1
                if ndep[v] == 0:
                    heapq.heappush(heaps[units[v][0]], (ready_t[v], -cp[v], v))
        order.sort()
        for _, u in order:
            self.replay(units[u][1])
        return max(efree.values())

    def replay(self, ops):
        for op in ops:
            eng, fn, reads, writes, dsem, final = op[:6]
            self.add(eng, fn, reads=reads, writes=writes, dsem=dsem, final=final)

    def add(self, eng, fn, reads=(), writes=(), dsem=None, final=False, dur=None):
        if self.defer is not None:
            self.defer.append((eng, fn, tuple(reads), tuple(writes), dsem, final, dur))
            return None
        deps = []
        for k in reads:
            r = self._get(k)
            if r is not None and r[0] is not None:
                deps.append((r[0], r[1]))
            elif r is not None and r[0] is None and r[2]:
                deps.extend(r[2])
            if r is not None and isinstance(k, tuple) and k[0] == "ps":
                deps.extend([x for x in r[2] if x[0][0] != eng])
        for k in writes:
            r = self._get(k)
            if r is not None:
                if r[0] is not None:
                    deps.append((r[0], r[1]))
                deps.extend(r[2])
        clk = self.clock[eng]
        best = {}
        for (tok, tclk) in deps:
            sk, v = tok
            if sk == eng and eng == "pe":
                continue
            if clk.get(sk, 0) >= v:
                continue
            if best.get(sk, 0) < v:
                best[sk] = v
            for s2, v2 in tclk.items():
                if clk.get(s2, 0) < v2:
                    clk[s2] = v2
            clk[sk] = max(clk.get(sk, 0), v)
        if dsem is not None and eng == "pool":
            hist = self.swdge_hist
            if len(hist) >= 8:
                ptok, pclk = hist[-8]
                if clk.get(ptok[0], 0) < ptok[1]:
                    best[ptok[0]] = max(best.get(ptok[0], 0), ptok[1])
                    for s2, v2 in pclk.items():
                        if clk.get(s2, 0) < v2:
                            clk[s2] = v2
                    clk[ptok[0]] = max(clk.get(ptok[0], 0), ptok[1])
        waits = list(best.items())
        if dsem is None:
            self.cnt[eng] += 1
            tok = (eng, self.cnt[eng])
        else:
            sk = ("d", dsem)
            self.dma_cnt[sk] = self.dma_cnt.get(sk, 0) + 16
            tok = (sk, self.dma_cnt[sk])
        myclk = dict(clk)
        myclk[tok[0]] = max(myclk.get(tok[0], 0), tok[1])
        if dsem is not None and eng == "pool":
            self.swdge_hist.append((tok, myclk))
        self.ops[eng].append((fn, waits, (tok[0], 16 if dsem is not None else 1)))
        for k in reads:
            r = self._get(k, create=True)
            r[2].append((tok, myclk))
        for k in writes:
            self.reg[k] = [tok, myclk, []]
        if final:
            self.final.append(tok)
        return tok

    def emit(self):
        nc = self.nc
        with ExitStack() as st:
            sems = {}
            for e in ENGS:
                sems[e] = st.enter_context(nc.semaphore("s_" + e))
            for sk in self.dma_cnt:
                sems[sk] = st.enter_context(nc.semaphore("s_d%d" % sk[1]))
            block = st.enter_context(nc.Block())
            deco = {"pe": block.tensor, "act": block.scalar, "dve": block.vector,
                    "pool": block.gpsimd, "sp": block.sync}
            final = list(self.final)

            def mk(e):
                def body(eng):
                    for fn, waits, inc in self.ops[e]:
                        for sk, v in waits:
                            eng.wait_ge(sems[sk], v)
                        fn(eng).then_inc(sems[inc[0]], inc[1])
                    if e == "sp":
                        best = {}
                        for sk, v in final:
                            best[sk] = max(best.get(sk, 0), v)
                        for sk, v in best.items():
                            eng.wait_ge(sems[sk], v)
                return body

            for e in ENGS:
                deco[e](mk(e))


R_NMIX, R_NMLP, R_NFIN, R_BPW1, R_WDW, R_BDW, R_LNG, R_LNB, R_BPW2 = 0, 2, 4, 5, 7, 38, 39, 40, 41
R_XMIX, R_W0, R_A0, R_KK, R_KA, R_RK, R_GNG, R_GNB = 42, 48, 49, 50, 51, 52, 53, 54
NVEC = 55


def build(stage=9):
    nc = bass.Bass("TRN2", target_bir_lowering=False)

    def din(name, shape):
        return nc.dram_tensor(name, shape, F32, kind="ExternalInput").ap()

    def dout(name, shape):
        return nc.dram_tensor(name, shape, F32, kind="ExternalOutput").ap()

    xp = din("xp", [2048, D]); xs = din("xs", [16, D]); sconv = din("sconv", [480, D])
    sshift = din("sshift", [16, D]); swkv = din("swkv", [128, 8192]); meta = din("meta", [16, D])
    vecs = din("vecs", [NVEC, D])
    w_pw1 = din("w_pw1", [D, 2 * D]); w_pw2 = din("w_pw2", [D, D])
    w_r = din("w_r", [D, D]); w_k = din("w_k", [D, D]); w_v = din("w_v", [D, D]); w_o = din("w_o", [D, D])
    w1 = din("w1", [D, 64]); w2 = din("w2", [64, D]); a1 = din("a1", [D, 64]); a2 = din("a2", [64, D])
    g1 = din("g1", [D, 128]); g2 = din("g2", [128, D])
    w_in = din("w_in", [2, D, 4 * D]); w_out = din("w_out", [2, 4 * D, D])

    y_p = dout("y_p", [2048, D]); y_s = dout("y_s", [16, D]); o_convp = dout("o_convp", [30, D])
    o_shiftp = dout("o_shiftp", [1, D]); o_wkvp = dout("o_wkvp", [16, 64, 64])
    o_convs = dout("o_convs", [480, D]); o_shifts = dout("o_shifts", [16, D]); o_wkvs = dout("o_wkvs", [128, 8192])

    scr1 = nc.dram_tensor("scr1", [112, D], F32, kind="Internal").ap()
    scr2 = nc.dram_tensor("scr2", [128, 128], F32, kind="Internal").ap()
    st = ExitStack()
    P = Prog(nc)

    def sb(name, shape, dt):
        return st.enter_context(nc.sbuf_tensor(name, shape, dt))

    hT = sb("hT", [128, 8, NG], F32)
    cT = sb("cT", [128, 8, 64], F32)
    cX = sb("cX", [128, 8, 16], F32)
    identf = sb("identf", [128, 128], F32)
    identb = sb("identb", [128, 128], BF16)
    onesb = sb("onesb", [128, 128], BF16)
    blkb = sb("blkb", [128, 128], BF16)
    onesf = sb("onesf", [128, 64], F32)
    m_su = sb("m_su", [128, 8, 64], BF16)
    m_il = sb("m_il", [128, 8, 64], BF16)
    m_sl = sb("m_sl", [128, 8, 64], BF16)
    m_id = sb("m_id", [128, 8, 64], BF16)
    Sst = sb("Sst", [128, 8, 64], F32)
    Sbf = sb("Sbf", [128, 8, 64], BF16)
    gtail = sb("gtail", [128, 8, 30], BF16)
    hlast = sb("hlast", [128, 8, 1], F32)
    shiftT = sb("shiftT", [128, 8, 16], F32)
    g32s = sb("g32s", [128, 8, 16], F32)
    g32p = sb("g32p", [128, 8, 30], F32)
    sampF = sb("sampF", [128, 8, 112], F32)
    CGp = sb("CGp", [128, 3, 128], F32)
    ring = [sb("ring%d" % i, [128, 4096], BF16) for i in range(2)]
    A1 = sb("A1", [128, 38400], BF16)
    A2 = sb("A2", [128, 39040], BF16)
    psall = st.enter_context(nc.psum_tensor("psall", [128, 4096], F32))
    ps = [psall[:, i * 512:(i + 1) * 512] for i in range(8)]

    class Arena:
        def __init__(self, t, name):
            self.t, self.name, self.off = t, name, 0

        def reset(self):
            P.fence(self.name)
            self.off = 0

        def take(self, shape, dt):
            n = int(np.prod(shape[1:]))
            nb = n * (2 if dt == BF16 else 4)
            nb = (nb + 63) // 64 * 64
            a = self.t[:, self.off // 2:(self.off + nb) // 2]
            self.off += nb
            assert self.off <= self.t.shape[1] * 2, (self.name, self.off)
            if dt == F32:
                a = a.bitcast(F32)
            a = a[:, 0:n]
            if len(shape) == 3:
                a = a.rearrange("p (k n) -> p k n", k=shape[1])
            elif len(shape) == 4:
                a = a.rearrange("p (a b c) -> p a b c", a=shape[1], b=shape[2])
            return a

    AR1 = Arena(A1, "A1")
    AR2 = Arena(A2, "A2")

    def cv(row, k=None):
        return cT[:, k, row:row + 1]

    P.add("pool", lambda e: e.memset(identf[:], 0.0), writes=["identf"])
    P.add("pool", lambda e: e.affine_select(out=identf[:], in_=identf[:], pattern=[[-1, 128]],
                                            compare_op=ALU.not_equal, fill=1.0, base=0, channel_multiplier=1),
          reads=["identf"], writes=["identf"])
    P.add("dve", lambda e: e.tensor_copy(out=identb[:], in_=identf[:]), reads=["identf"], writes=["identb"])
    P.add("dve", lambda e: e.memset(onesb[:], 1.0), writes=["onesb"])
    P.add("dve", lambda e: e.memset(onesf[:], 1.0), writes=["onesf"])
    P.add("dve", lambda e: e.memset(blkb[:], 0.0), writes=["blkb"])
    P.add("dve", lambda e: e.memset(blkb[0:64, 0:64], 1.0), reads=["blkb"], writes=["blkb"])
    P.add("dve", lambda e: e.memset(blkb[64:128, 64:128], 1.0), reads=["blkb"], writes=["blkb"])
    AR2.reset()
    onesv = AR2.take([128, 8, 64], F32)
    P.add("dve", lambda e: e.memset(onesv, 1.0), writes=[("A2", "onesv")])
    for (m, nm, op, cm, stp) in ((m_su, "m_su", ALU.is_gt, -1, 1), (m_il, "m_il", ALU.is_ge, -1, 1),
                                 (m_sl, "m_sl", ALU.is_gt, 1, -1), (m_id, "m_id", ALU.is_equal, 1, -1)):
        for h in range(2):
            P.add("pool", lambda e, m=m, op=op, cm=cm, stp=stp, h=h: e.affine_select(
                out=m[h * 64:(h + 1) * 64], in_=onesv[h * 64:(h + 1) * 64], pattern=[[0, 8], [stp, 64]],
                compare_op=op, fill=0.0, base=0, channel_multiplier=cm),
                reads=[("A2", "onesv")], writes=[(nm, h)])
    MASKR = lambda nm: [(nm, 0), (nm, 1)]

    crow = AR2.take([128, 1024], F32)
    P.add("sp", lambda e: e.dma_start(out=crow[0:NVEC, :], in_=vecs[:, :]), writes=[("A2", "crow")], dsem=8)
    for k in range(8):
        P.add("pe", lambda e, k=k: e.transpose(out=ps[0][:, k * 64:k * 64 + NVEC], in_=crow[0:NVEC, k * 128:(k + 1) * 128],
                                               identity=identf[0:NVEC, 0:NVEC]),
              reads=[("A2", "crow"), "identf"], writes=[("ps", 0)])
    P.add("dve", lambda e: e.tensor_copy(out=cT[:, :, 0:NVEC], in_=ps[0][:].rearrange("p (k n) -> p k n", k=8)[:, :, 0:NVEC]),
          reads=[("ps", 0)], writes=["cT"])
    P.add("dve", lambda e: e.tensor_scalar(out=cX[:, :, 0:6], in0=cT[:, :, R_XMIX:R_XMIX + 6], scalar1=-1.0, scalar2=1.0,
                                           op0=ALU.mult, op1=ALU.add), reads=["cT"], writes=["cX"])
    P.add("dve", lambda e: e.tensor_scalar(out=cX[:, :, 6:7], in0=cT[:, :, R_W0:R_W0 + 1], scalar1=-1.0, scalar2=None,
                                           op0=ALU.mult), reads=["cT", "cX"], writes=["cX"])
    for ci, row in enumerate((R_GNG, R_GNB, R_RK)):
        for t in range(16):
            P.add("sp", lambda e, t=t, ci=ci, row=row: e.dma_start(out=CGp[t * 8:(t + 1) * 8, ci, :], in_=vecs[row].rearrange("(j e) -> j e", e=128)),
                  writes=[("CGp", ci, t)], dsem=23)
    _l = P.reg[("CGp", 2, 15)]
    P.reg["CGp"] = [_l[0], _l[1], []]
    P.add("dve", lambda e: e.memset(Sst[:], 0.0), writes=["Sst"])
    P.add("dve", lambda e: e.memset(Sbf[:], 0.0), writes=["Sbf"])
    P.add("dve", lambda e: e.memset(gtail[:], 0.0), writes=["gtail"])
    P.add("dve", lambda e: e.memset(hlast[:], 0.0), writes=["hlast"])

    rr = {"i": 0}

    def ring_fill(dmas):
        s = rr["i"] % 2
        rr["i"] += 1
        for di, mk in enumerate(dmas):
            o, i = mk(ring[s])
            wk = [("ring", s)] if di == 0 else [("ringg", s)]
            if len(dmas) == 1:
                wk = [("ring", s), ("ringg", s)]
            P.add("pool", lambda e, o=o, i=i: e.dma_start(out=o, in_=i), writes=wk, dsem=s if di == 0 else 2 + s)
        return s

    def tiles_of(n, step=352):
        out, c = [], 0
        while c < n:
            m = min(step, n - c)
            out.append((c, m))
            c += m
        return out

    TL = tiles_of(NG)
    psr = {"lo": 0, "hi": 8}
    psc = {}

    def next_ps():
        key = (psr["lo"], psr["hi"])
        c = psc.get(key, 0)
        psc[key] = c + 1
        return key[0] + c % (key[1] - key[0])

    def rmsnorm(src, n, grow, dst, tmp, tmpkey, extra_reads=(), srckey="hT", tmps=None, lnexp=False, tilekeys=False):
        if tmps is None:
            sq = tmp.take([128, 2, 352], BF16)
            rs = tmp.take([128, 352], F32)
        else:
            sq, rs = tmps
        for (t0, m) in tiles_of(n):
            b = next_ps()
            for k in range(8):
                P.add("act", lambda e, k=k, t0=t0, m=m: e.activation(out=sq[:, k % 2, 0:m], in_=src[:, k, t0:t0 + m], func=AF.Square),
                      reads=[srckey], writes=[(tmpkey, "sq", k % 2)])
                P.add("pe", lambda e, k=k, m=m, b=b: e.matmul(ps[b][:, 0:m], lhsT=onesb[:], rhs=sq[:, k % 2, 0:m], start=(k == 0), stop=(k == 7)),
                      reads=[(tmpkey, "sq", k % 2), "onesb"], writes=[("ps", b)])
            if lnexp:
                P.add("act", lambda e, m=m, b=b: e.activation(out=rs[:, 0:m], in_=ps[b][:, 0:m], func=AF.Ln, scale=1.0 / D, bias=RMS_EPS),
                      reads=[("ps", b)], writes=[(tmpkey, "rs")])
                P.add("act", lambda e, m=m: e.activation(out=rs[:, 0:m], in_=rs[:, 0:m], func=AF.Exp, scale=-0.5), reads=[(tmpkey, "rs")], writes=[(tmpkey, "rs")])
            else:
                P.add("act", lambda e, m=m, b=b: e.activation(out=rs[:, 0:m], in_=ps[b][:, 0:m], func=AF.Sqrt, scale=1.0 / D, bias=RMS_EPS),
                      reads=[("ps", b)], writes=[(tmpkey, "rs")])
                P.add("dve", lambda e, m=m: e.reciprocal(out=rs[:, 0:m], in_=rs[:, 0:m]), reads=[(tmpkey, "rs")], writes=[(tmpkey, "rs")])
            for k in range(8):
                P.add("dve", lambda e, k=k, t0=t0, m=m: e.scalar_tensor_tensor(
                    out=dst[:, k, t0:t0 + m], in0=src[:, k, t0:t0 + m], scalar=cv(grow, k), in1=rs[:, 0:m], op0=ALU.mult, op1=ALU.mult),
                    reads=[srckey, (tmpkey, "rs"), "cT"], writes=[(tmpkey, "dst", k, t0) if tilekeys else (tmpkey, "dst", k)])

    wv = lambda w: w.rearrange("(k p) n -> p k n", p=128)


    def rwkv(g):
        c0 = g * NG
        AR1.reset()
        AR2.reset()
        Wr = AR1.take([128, 8, 1024], BF16); Wk = AR1.take([128, 8, 1024], BF16)
        Wv = AR1.take([128, 8, 1024], BF16); Wo = AR1.take([128, 8, 1024], BF16)
        W1 = AR1.take([128, 8, 64], BF16); A1w = AR1.take([128, 8, 64], BF16); G1 = AR1.take([128, 8, 128], BF16)
        W2 = AR1.take([128, 1024], BF16); A2w = AR1.take([128, 1024], BF16); G2 = AR1.take([128, 1024], BF16)
        wl = [(Wr, wv(w_r), "Wr"), (Wk, wv(w_k), "Wk"), (Wv, wv(w_v), "Wv"), (W1, wv(w1), "W1"), (W2[0:64, :], w2[:, :], "W2"),
              (A1w, wv(a1), "A1w"), (A2w[0:64, :], a2[:, :], "A2w"), (G1, wv(g1), "G1"), (G2[:, :], g2[:, :], "G2"), (Wo, wv(w_o), "Wo")]
        for wi, (dst_, src_, nm) in enumerate(wl):
            P.add("pool", lambda e, d=dst_, s_=src_: e.dma_start(out=d, in_=s_), writes=[("A1", nm)], dsem=30 + wi)

        T = 64
        n = 64
        f32 = lambda nm, *sh: AR2.take([128] + list(sh), F32)
        b16 = lambda nm, *sh: AR2.take([128] + list(sh), BF16)
        hnw = f32("hnw", 8, T + 1); xx = f32("xx", 8, T)
        xm = [b16("xm", 8, T) for _ in range(2)]
        vT = b16("vT", 8, T); a32 = f32("a32", 8, T)
        rgf = lambda s_, o_: ring[s_][:, o_ * 1024:(o_ + 1) * 1024].bitcast(F32).rearrange("p (k n) -> p k n", k=8)
        rgb = lambda s_, o_: ring[s_][:, o_:o_ + 512].rearrange("p (k n) -> p k n", k=8)
        IF1 = [dict(r32=f32("r32", 8, T), k32=f32("k32", 8, T), kk32=f32("kk32", 8, T), b32=f32("b32", 8, T), ew=f32("ew", 8, T)),
               dict(r32=rgf(0, 0), k32=rgf(0, 1), kk32=rgf(0, 2), b32=rgf(0, 3), ew=rgf(1, 0))]
        tmpA = f32("tmpA", 8, T); tmpB = b16("tmpB", 8, T)
        lo1 = b16("lo1", T); lo1f = f32("lo1f", T)
        cs = f32("cs", 8, 64); e1 = f32("e1", 8, 64); e2 = f32("e2", 8, 64); d1 = f32("d1", 8, 64)
        kt = b16("kt", 8, 64); bt = b16("bt", 8, 64); kPC = b16("kPC", 8, 64); bPC = b16("bPC", 8, 64)
        Mb = [b16("M", 8, 64) for _ in range(2)]; MTb = [b16("MT", 8, 64) for _ in range(2)]; Rb = [b16("R", 8, 64) for _ in range(2)]
        nsq = AR2.take([128, 2, 64], BF16); nrs = AR2.take([128, 64], F32)
        IF = []
        for i_ in range(2):
            IF.append(dict(rt=b16("rt", 8, 64), at=b16("at", 8, 64), Aak=b16("Aak", 8, 64), Arb=b16("Arb", 8, 64), Ark=b16("Ark", 8, 64),
                           RF=b16("RF", 8, 64), kPCt=b16("kPCt", 8, 64), bPCt=b16("bPCt", 8, 64), PCc=f32("PCc", 8, 1)))
        IF3 = [dict(Vtok=b16("Vtok", 8, 64), gT=b16("gT", 8, 64), bonT=f32("bonT", 8, 64)) for _ in range(2)]
        IF3.append(dict(Vtok=rgb(1, 1024), gT=rgb(1, 1536), bonT=rgf(1, 2)))
        Wb = b16("Wb", 8, 64); Ub = b16("Ub", 8, 64)
        scrS = f32("scrS", 1024)
        ysq = scrS[:, 0:512].rearrange("p (k n) -> p k n", k=8); yc = scrS[:, 512:1024].rearrange("p (k n) -> p k n", k=8)
        yh = b16("yh", 8, 64); z1 = f32("z1", 8, 64); st8 = f32("st8", 8, 8); zT = b16("zT", 8, T)
        K2 = lambda nm: ("A2", nm)
        v3 = lambda ap, k=8: ap.rearrange("p (k n) -> p k n", k=k)
        if g == 0:
            print("rwkv A2 bytes used", AR2.off, "of", A2.shape[1] * 2)

        def ev(eng, out, in_, reads, writes):
            if eng == "act":
                P.add("act", lambda e: e.activation(out=out, in_=in_, func=AF.Copy), reads=reads, writes=writes)
            else:
                P.add(eng, lambda e: e.tensor_copy(out=out, in_=in_), reads=reads, writes=writes)

        def tt(eng, out, in0, in1, op, reads, writes):
            P.add(eng, lambda e: e.tensor_tensor(out=out, in0=in0, in1=in1, op=op), reads=reads, writes=writes)

        def act(out, in_, func, reads, writes, **kw):
            P.add("act", lambda e: e.activation(out=out, in_=in_, func=func, **kw), reads=reads, writes=writes)

        def proj(Wt, wkey, xin, xkey):
            b = next_ps()
            for o in range(8):
                for k in range(8):
                    P.add("pe", lambda e, b=b, o=o, k=k: e.matmul(ps[b][:, o * 64:(o + 1) * 64], lhsT=Wt[:, k, o * 128:(o + 1) * 128],
                                                                  rhs=xin[:, k, 0:n], start=(k == 0), stop=(k == 7)),
                          reads=[wkey, xkey], writes=[("ps", b)])
            return b

        def blockmm(b, pairs, reads):
            for par in range(2):
                for j in range(8):
                    for pi, (L, Rr) in enumerate(pairs):
                        P.add("pe", lambda e, j=j, par=par, L=L, Rr=Rr, pi=pi: e.matmul(
                            ps[b][par * 64:(par + 1) * 64, j * 64:(j + 1) * 64], lhsT=L[par * 64:(par + 1) * 64, j, :],
                            rhs=Rr[par * 64:(par + 1) * 64, j, :], start=(pi == 0), stop=(pi == len(pairs) - 1)),
                            reads=reads, writes=[("ps", b)])

        tl = tiles_of(NG, T)
        NTI = len(tl)
        base = P.reg.get("hT")
        for ti in range(NTI):
            if base is not None:
                P.reg[("hT", ti)] = [base[0], base[1], list(base[2])]

        def gen_P1(ti):
            tc0 = tl[ti][0]
            I = IF[ti % 2]
            IK = lambda nm: K2((nm, ti % 2))
            J = IF1[ti % 2]
            JK = lambda nm: K2((nm, "j", ti % 2))
            Q = IF3[ti % 3]
            QK = lambda nm: K2((nm, "q", ti % 3))
            HK = ("hT", ti)
            first = (g == 0 and ti == 0)
            rt, at, Aak, Arb, Ark, RF, kPCt, bPCt, PCc = (I[x] for x in ("rt", "at", "Aak", "Arb", "Ark", "RF", "kPCt", "bPCt", "PCc"))
            r32, k32, kk32, b32, ew = (J[x] for x in ("r32", "k32", "kk32", "b32", "ew"))
            Vtok, gT, bonT = (Q[x] for x in ("Vtok", "gT", "bonT"))
            P.add("pool", lambda e: e.tensor_copy(out=hnw[:, :, 0:1], in_=hlast[:]), reads=["hlast"], writes=[K2("hnw")])
            rmsnorm(hT[:, :, tc0:tc0 + n], n, R_NMIX + 1, hnw[:, :, 1:1 + n], AR2, "A2n", tmps=(nsq, nrs), srckey=HK, lnexp=True)
            HNW = [K2("hnw")] + [("A2n", "dst", k) for k in range(8)]
            if first:
                P.add("dve", lambda e: e.memset(hnw[:, :, 17:49], 0.0), reads=HNW, writes=HNW)
            P.add("pool", lambda e: e.tensor_copy(out=hlast[:], in_=hnw[:, :, n:n + 1]), reads=HNW, writes=["hlast"])
            if first:
                for h in range(2):
                    b = next_ps()
                    for kk_ in range(4):
                        k = h * 4 + kk_
                        P.add("pe", lambda e, k=k, kk_=kk_, b=b: e.transpose(out=ps[b][0:16, kk_ * 128:(kk_ + 1) * 128], in_=hnw[:, k, 1:17], identity=identf[:]),
                              reads=HNW + ["identf"], writes=[("ps", b)])
                    ev("act", scrS[0:16, h * 512:(h + 1) * 512], ps[b][0:16, :], [("ps", b)], [K2("scrS")])
                P.add("sp", lambda e: e.dma_start(out=o_shifts[:, :], in_=scrS[0:16, :]), reads=[K2("scrS")], writes=["o_shifts"], dsem=14, final=True)
            if g == NGRP - 1 and ti == NTI - 1:
                P.add("sp", lambda e: e.dma_start(out=o_shiftp.rearrange("o (k p) -> p (o k)", p=128), in_=hnw[:, :, n:n + 1].rearrange("p k o -> p (k o)"),
                                                  allow_slow_non_contiguous=True), reads=HNW, writes=["o_shiftp"], dsem=14, final=True)
            tt("pool", xx[:, :, 0:n], hnw[:, :, 0:n], hnw[:, :, 1:1 + n], ALU.subtract, HNW, [K2("xx")])
            if first:
                tt("pool", xx[:, :, 0:16], shiftT[:], hnw[:, :, 1:17], ALU.subtract, HNW + ["shiftT", K2("xx")], [K2("xx")])
            mixi = {"r": 0, "w": 1, "k": 2, "v": 3, "a": 4, "g": 5}
            xcnt = {"i": 0}

            def mix(nm):
                i = xcnt["i"] % 2
                xcnt["i"] += 1
                m = mixi[nm]
                tt("pool", tmpA[:, :, 0:n], xx[:, :, 0:n], cT[:, :, R_XMIX + m:R_XMIX + m + 1].to_broadcast([128, 8, n]), ALU.mult,
                   [K2("xx"), "cT"], [K2("tmpA")])
                tt("dve", xm[i][:, :, 0:n], tmpA[:, :, 0:n], hnw[:, :, 1:1 + n], ALU.add, [K2("tmpA")] + HNW, [K2(("xm", i))])
                return xm[i], K2(("xm", i))

            xr, xrk = mix("r")
            b = proj(Wr, ("A1", "Wr"), xr, xrk)
            ev("act", r32, v3(ps[b][:]), [("ps", b)], [JK("r32")])
            xk_, xkk = mix("k")
            b = proj(Wk, ("A1", "Wk"), xk_, xkk)
            ev("act", k32, v3(ps[b][:]), [("ps", b)], [JK("k32")])
            xv, xvk = mix("v")
            b = proj(Wv, ("A1", "Wv"), xv, xvk)
            ev("act", vT, v3(ps[b][:]), [("ps", b)], [K2("vT")])
            Wv4 = Wv.rearrange("p k (j a c) -> p k j a c", a=2, c=64)
            b = next_ps()
            for par in range(2):
                for k in range(8):
                    P.add("pe", lambda e, b=b, par=par, k=k: e.matmul(
                        ps[b][par * 64:(par + 1) * 64, :], lhsT=xv[:, k, 0:64], rhs=Wv4[:, k, :, par, :],
                        start=(k == 0), stop=(k == 7)), reads=[("A1", "Wv"), xvk], writes=[("ps", b)])
            ev("act", Vtok, v3(ps[b][:]), [("ps", b)], [QK("Vtok")])
            if first:
                P.add("dve", lambda e: e.memset(Vtok[0:48], 0.0), reads=[QK("Vtok")], writes=[QK("Vtok")])
                P.add("dve", lambda e: e.memset(Vtok[64:112], 0.0), reads=[QK("Vtok")], writes=[QK("Vtok")])

            def lora(nm, W1t, w1key, W2t, w2key, rank, func1):
                xin, xkey = mix(nm)
                b = next_ps()
                for k in range(8):
                    P.add("pe", lambda e, b=b, k=k: e.matmul(ps[b][0:rank, 0:n], lhsT=W1t[:, k, :], rhs=xin[:, k, 0:n], start=(k == 0), stop=(k == 7)),
                          reads=[w1key, xkey], writes=[("ps", b)])
                if func1 == "tanh":
                    act(lo1f[0:rank, 0:n], ps[b][0:rank, 0:n], AF.Exp, [("ps", b)], [K2("lo1f")], scale=2.0)
                    P.add("dve", lambda e: e.tensor_scalar(out=lo1f[0:rank, 0:n], in0=lo1f[0:rank, 0:n], scalar1=1.0, scalar2=None, op0=ALU.add), reads=[K2("lo1f")], writes=[K2("lo1f")])
                    P.add("dve", lambda e: e.reciprocal(out=lo1f[0:rank, 0:n], in_=lo1f[0:rank, 0:n]), reads=[K2("lo1f")], writes=[K2("lo1f")])
                    P.add("dve", lambda e: e.tensor_scalar(out=lo1[0:rank, 0:n], in0=lo1f[0:rank, 0:n], scalar1=-2.0, scalar2=1.0, op0=ALU.mult, op1=ALU.add),
                          reads=[K2("lo1f")], writes=[K2("lo1")])
                elif func1 == "sigmoid":
                    act(lo1f[0:rank, 0:n], ps[b][0:rank, 0:n], AF.Exp, [("ps", b)], [K2("lo1f")], scale=-1.0)
                    P.add("dve", lambda e: e.tensor_scalar(out=lo1f[0:rank, 0:n], in0=lo1f[0:rank, 0:n], scalar1=1.0, scalar2=None, op0=ALU.add), reads=[K2("lo1f")], writes=[K2("lo1f")])
                    P.add("dve", lambda e: e.reciprocal(out=lo1f[0:rank, 0:n], in_=lo1f[0:rank, 0:n]), reads=[K2("lo1f")], writes=[K2("lo1f")])
                    P.add("dve", lambda e: e.tensor_copy(out=lo1[0:rank, 0:n], in_=lo1f[0:rank, 0:n]), reads=[K2("lo1f")], writes=[K2("lo1")])
                else:
                    act(lo1[0:rank, 0:n], ps[b][0:rank, 0:n], func1, [("ps", b)], [K2("lo1")])
                b2 = next_ps()
                for o in range(8):
                    P.add("pe", lambda e, b2=b2, o=o: e.matmul(ps[b2][:, o * 64:(o + 1) * 64], lhsT=W2t[0:rank, o * 128:(o + 1) * 128],
                                                               rhs=lo1[0:rank, 0:n], start=True, stop=True),
                          reads=[w2key, K2("lo1")], writes=[("ps", b2)])
                return b2

            b = lora("w", W1, ("A1", "W1"), W2, ("A1", "W2"), 64, AF.Tanh)
            tt("dve", ew, v3(ps[b][:]), cT[:, :, R_W0:R_W0 + 1].to_broadcast([128, 8, n]), ALU.add, [("ps", b), "cT"], [JK("ew")])
            act(ew, ew, AF.Exp, [JK("ew")], [JK("ew")], scale=-1.0)
            act(ew, ew, AF.Ln, [JK("ew")], [JK("ew")], bias=1.0)
            act(ew, ew, AF.Exp, [JK("ew")], [JK("ew")], scale=-1.0, bias=-0.5)
            b = lora("a", A1w, ("A1", "A1w"), A2w, ("A1", "A2w"), 64, AF.Copy)
            tt("dve", a32, v3(ps[b][:]), cT[:, :, R_A0:R_A0 + 1].to_broadcast([128, 8, n]), ALU.add, [("ps", b), "cT"], [K2("a32")])
            act(a32, a32, AF.Exp, [K2("a32")], [K2("a32")], scale=-1.0)
            act(a32, a32, AF.Ln, [K2("a32")], [K2("a32")], bias=1.0)
            act(a32, a32, AF.Exp, [K2("a32")], [K2("a32")], scale=-1.0)
            b = lora("g", G1, ("A1", "G1"), G2, ("A1", "G2"), 128, AF.Sigmoid)
            ev("act", gT, v3(ps[b][:]), [("ps", b)], [QK("gT")])
            tt("pool", kk32, k32, cT[:, :, R_KK:R_KK + 1].to_broadcast([128, 8, n]), ALU.mult, [JK("k32"), "cT"], [JK("kk32")])
            act(tmpB, kk32, AF.Square, [JK("kk32")], [K2("tmpB")])
            b = next_ps()
            for o in range(8):
                P.add("pe", lambda e, b=b, o=o: e.matmul(ps[b][:, o * 64:(o + 1) * 64], lhsT=blkb[:], rhs=tmpB[:, o, :], start=True, stop=True),
                      reads=["blkb", K2("tmpB")], writes=[("ps", b)])
            P.add("dve", lambda e, b=b: e.tensor_scalar(out=tmpA, in0=v3(ps[b][:]), scalar1=1e-18, scalar2=None, op0=ALU.max), reads=[("ps", b), K2("tmpA")], writes=[K2("tmpA")])
            act(tmpA, tmpA, AF.Ln, [K2("tmpA")], [K2("tmpA")])
            act(tmpA, tmpA, AF.Exp, [K2("tmpA")], [K2("tmpA")], scale=-0.5)
            tt("dve", kk32, kk32, tmpA, ALU.mult, [JK("kk32"), K2("tmpA")], [JK("kk32")])
            P.add("pool", lambda e: e.tensor_scalar(out=tmpA, in0=a32, scalar1=1.0, scalar2=-1.0, op0=ALU.mult, op1=ALU.add), reads=[K2("a32"), K2("tmpA")], writes=[K2("tmpA")])
            tt("pool", tmpA, tmpA, cT[:, :, R_KA:R_KA + 1].to_broadcast([128, 8, n]), ALU.mult, [K2("tmpA"), "cT"], [K2("tmpA")])
            P.add("dve", lambda e: e.scalar_tensor_tensor(out=k32, in0=tmpA, scalar=1.0, in1=k32, op0=ALU.add, op1=ALU.mult),
                  reads=[K2("tmpA"), JK("k32"), JK("kk32")], writes=[JK("k32")])
            tt("pool", b32, kk32, a32, ALU.mult, [JK("kk32"), K2("a32")], [JK("b32")])
            tt("pool", tmpA, r32, cT[:, :, R_RK:R_RK + 1].to_broadcast([128, 8, n]), ALU.mult, [JK("r32"), K2("tmpA"), "cT"], [K2("tmpA")])
            tt("dve", tmpB, tmpA, k32, ALU.mult, [K2("tmpA"), JK("k32"), K2("tmpB")], [K2("tmpB")])
            b = next_ps()
            for o in range(8):
                P.add("pe", lambda e, b=b, o=o: e.matmul(ps[b][:, o * 64:(o + 1) * 64], lhsT=blkb[:], rhs=tmpB[:, o, :], start=True, stop=True),
                      reads=["blkb", K2("tmpB")], writes=[("ps", b)])
            tt("dve", bonT, v3(ps[b][:]), vT, ALU.mult, [("ps", b), K2("vT")], [QK("bonT")])
            if first:
                for mi, (src_, key_) in enumerate(((kk32, JK("kk32")), (None, None), (b32, JK("b32")), (vT, K2("vT")), (k32, JK("k32")), (r32, JK("r32")), (gT, QK("gT")))):
                    if src_ is None:
                        act(sampF[:, :, 16:32], ew[:, :, 0:16], AF.Exp, [JK("ew")], ["sampF"], scale=-1.0)
                    else:
                        P.add("pool", lambda e, mi=mi, src_=src_: e.tensor_copy(out=sampF[:, :, mi * 16:(mi + 1) * 16], in_=src_[:, :, 0:16]),
                              reads=[key_], writes=["sampF"])
        def gen_P2(ti):
            tc0 = tl[ti][0]
            I = IF[ti % 2]
            IK = lambda nm: K2((nm, ti % 2))
            J = IF1[ti % 2]
            JK = lambda nm: K2((nm, "j", ti % 2))
            Q = IF3[ti % 3]
            QK = lambda nm: K2((nm, "q", ti % 3))
            HK = ("hT", ti)
            first = (g == 0 and ti == 0)
            rt, at, Aak, Arb, Ark, RF, kPCt, bPCt, PCc = (I[x] for x in ("rt", "at", "Aak", "Arb", "Ark", "RF", "kPCt", "bPCt", "PCc"))
            r32, k32, kk32, b32, ew = (J[x] for x in ("r32", "k32", "kk32", "b32", "ew"))
            Vtok, gT, bonT = (Q[x] for x in ("Vtok", "gT", "bonT"))
            for j in range(8):
                P.add("dve", lambda e, j=j: e.tensor_tensor_scan(out=cs[:, j, :], data0=onesf[:, 0:64], data1=ew[:, j, :], initial=0.0,
                                                                 op0=ALU.mult, op1=ALU.add), reads=[JK("ew"), "onesf"], writes=[K2("cs")])
            act(e1, cs, AF.Exp, [K2("cs")], [K2("e1")], scale=-1.0)
            tt("dve", rt, r32, e1, ALU.mult, [JK("r32"), K2("e1")], [IK("rt")])
            act(e2, cs, AF.Exp, [K2("cs")], [K2("e2")])
            tt("dve", kt, k32, e2, ALU.mult, [JK("k32"), K2("e2")], [K2("kt")])
            tt("dve", bt, b32, e2, ALU.mult, [JK("b32"), K2("e2")], [K2("bt")])
            tt("pool", d1, cs, ew, ALU.subtract, [K2("cs"), JK("ew")], [K2("d1")])
            act(e1, d1, AF.Exp, [K2("d1"), K2("e1")], [K2("e1")], scale=-1.0)
            P.add("dve", lambda e: e.scalar_tensor_tensor(out=at, in0=kk32, scalar=-1.0, in1=e1, op0=ALU.mult, op1=ALU.mult),
                  reads=[JK("kk32"), K2("e1")], writes=[IK("at")])
            tt("pool", d1, cs, cs[:, :, 63:64].to_broadcast([128, 8, 64]), ALU.subtract, [K2("cs"), K2("d1")], [K2("d1")])
            act(e2, d1, AF.Exp, [K2("d1"), K2("e2")], [K2("e2")])
            tt("pool", kPC, k32, e2, ALU.mult, [JK("k32"), K2("e2")], [K2("kPC")])
            tt("pool", bPC, b32, e2, ALU.mult, [JK("b32"), K2("e2")], [K2("bPC")])
            act(PCc, cs[:, :, 63:64], AF.Exp, [K2("cs")], [IK("PCc")], scale=-1.0)
            if first:
                for (t_, key_, eng_) in ((rt, IK("rt"), "dve"), (kt, K2("kt"), "dve"), (at, IK("at"), "dve"), (bt, K2("bt"), "dve"),
                                         (kPC, K2("kPC"), "pool"), (bPC, K2("bPC"), "pool")):
                    P.add(eng_, lambda e, t_=t_: e.memset(t_[:, :, 0:48], 0.0), reads=[key_], writes=[key_])
            bT = next_ps()
            pT = ps[bT][:].bitcast(BF16)
            for ui, (src_, nm) in enumerate(((kPC, "kPC"), (bPC, "bPC"))):
                for par in range(2):
                    for j in range(8):
                        P.add("pe", lambda e, ui=ui, src_=src_, j=j, par=par: e.transpose(
                            out=pT[par * 64:(par + 1) * 64, ui * 512 + j * 64:ui * 512 + (j + 1) * 64], in_=src_[par * 64:(par + 1) * 64, j, :],
                            identity=identb[par * 64:(par + 1) * 64, par * 64:(par + 1) * 64]), reads=[K2(nm), "identb"], writes=[("ps", bT)])
            ev("act", kPCt, v3(pT[:, 0:512]), [("ps", bT)], [IK("kPCt")])
            ev("act", bPCt, v3(pT[:, 512:1024]), [("ps", bT)], [IK("bPCt")])
            bN = next_ps(); blockmm(bN, [(bt, at)], [K2("bt"), IK("at")])
            tt("dve", Mb[0], v3(ps[bN][:]), m_su[:], ALU.mult, [("ps", bN)] + MASKR("m_su"), [K2(("M", 0))])
            bNT = next_ps(); blockmm(bNT, [(at, bt)], [K2("bt"), IK("at")])
            tt("dve", MTb[0], v3(ps[bNT][:]), m_sl[:], ALU.mult, [("ps", bNT)] + MASKR("m_sl"), [K2(("MT", 0))])
            b_ = next_ps(); blockmm(b_, [(kt, at)], [K2("kt"), IK("at")])
            tt("dve", Aak, v3(ps[b_][:]), m_su[:], ALU.mult, [("ps", b_)] + MASKR("m_su"), [IK("Aak")])
            b_ = next_ps(); blockmm(b_, [(bt, rt)], [K2("bt"), IK("rt")])
            tt("dve", Arb, v3(ps[b_][:]), m_il[:], ALU.mult, [("ps", b_)] + MASKR("m_il"), [IK("Arb")])
            b_ = next_ps(); blockmm(b_, [(kt, rt)], [K2("kt"), IK("rt")])
            tt("dve", Ark, v3(ps[b_][:]), m_il[:], ALU.mult, [("ps", b_)] + MASKR("m_il"), [IK("Ark")])
            tt("pool", Rb[0], Mb[0], m_id[:], ALU.add, [K2(("M", 0))] + MASKR("m_id"), [K2(("R", 0))])
            cur = 0
            for lvl in range(1, 6):
                nxt = 1 - cur
                if lvl < 5:
                    b_ = next_ps(); blockmm(b_, [(MTb[cur], Mb[cur])], [K2(("M", cur)), K2(("MT", cur))])
                    ev("act", Mb[nxt], v3(ps[b_][:]), [("ps", b_)], [K2(("M", nxt))])
                b_ = next_ps(); blockmm(b_, [(Mb[cur], MTb[cur])], [K2(("M", cur)), K2(("MT", cur))])
                ev("act", MTb[nxt], v3(ps[b_][:]), [("ps", b_)], [K2(("MT", nxt))])
                b_ = next_ps(); blockmm(b_, [(MTb[nxt], Rb[cur])], [K2(("MT", nxt)), K2(("R", cur))])
                if lvl < 5:
                    tt("dve", Rb[nxt], v3(ps[b_][:]), Rb[cur], ALU.add, [("ps", b_), K2(("R", cur))], [K2(("R", nxt))])
                else:
                    tt("dve", RF, v3(ps[b_][:]), Rb[cur], ALU.add, [("ps", b_), K2(("R", cur))], [IK("RF")])
                cur = nxt

        def gen_S(ti):
            tc0 = tl[ti][0]
            I = IF[ti % 2]
            IK = lambda nm: K2((nm, ti % 2))
            J = IF1[ti % 2]
            JK = lambda nm: K2((nm, "j", ti % 2))
            Q = IF3[ti % 3]
            QK = lambda nm: K2((nm, "q", ti % 3))
            HK = ("hT", ti)
            first = (g == 0 and ti == 0)
            rt, at, Aak, Arb, Ark, RF, kPCt, bPCt, PCc = (I[x] for x in ("rt", "at", "Aak", "Arb", "Ark", "RF", "kPCt", "bPCt", "PCc"))
            r32, k32, kk32, b32, ew = (J[x] for x in ("r32", "k32", "kk32", "b32", "ew"))
            Vtok, gT, bonT = (Q[x] for x in ("Vtok", "gT", "bonT"))
            VK = QK("Vtok")
            bW = next_ps(); blockmm(bW, [(at, Sbf), (Aak, Vtok)], [IK("at"), "Sbf", IK("Aak"), VK])
            ev("act", Wb, v3(ps[bW][:]), [("ps", bW)], [K2("Wb")])
            bU = next_ps(); blockmm(bU, [(RF, Wb)], [IK("RF"), K2("Wb")])
            ev("act", Ub, v3(ps[bU][:]), [("ps", bU)], [K2("Ub")])
            bY = next_ps(); blockmm(bY, [(rt, Sbf), (Arb, Ub), (Ark, Vtok)], [IK("rt"), "Sbf", IK("Arb"), K2("Ub"), IK("Ark"), VK])
            bS = next_ps(); blockmm(bS, [(bPCt, Ub), (kPCt, Vtok)], [IK("bPCt"), K2("Ub"), IK("kPCt"), VK])
            tt("pool", Sst[:], Sst[:], PCc.to_broadcast([128, 8, 64]), ALU.mult, ["Sst", IK("PCc")], ["Sst"])
            tt("dve", Sst[:], v3(ps[bS][:]), Sst[:], ALU.add, [("ps", bS), "Sst"], ["Sst"])
            ev("act", Sbf[:], Sst[:], ["Sst"], ["Sbf"])
            pY = v3(ps[bY][:])
            SK = K2("scrS")
            P.add("dve", lambda e, pY=pY: e.tensor_reduce(out=st8[:, :, 0], in_=pY, axis=AX.X, op=ALU.add), reads=[("ps", bY)], writes=[K2("st8")])
            act(ysq, pY, AF.Square, [("ps", bY)], [SK])
            P.add("dve", lambda e: e.tensor_reduce(out=st8[:, :, 1], in_=ysq, axis=AX.X, op=ALU.add), reads=[SK, K2("st8")], writes=[K2("st8")])
            P.add("dve", lambda e: e.tensor_scalar(out=st8[:, :, 0:2], in0=st8[:, :, 0:2], scalar1=1.0 / 64, scalar2=None, op0=ALU.mult), reads=[K2("st8")], writes=[K2("st8")])
            tt("dve", st8[:, :, 2], st8[:, :, 0], st8[:, :, 0], ALU.mult, [K2("st8")], [K2("st8")])
            tt("dve", st8[:, :, 3], st8[:, :, 1], st8[:, :, 2], ALU.subtract, [K2("st8")], [K2("st8")])
            act(st8[:, :, 3], st8[:, :, 3], AF.Ln, [K2("st8")], [K2("st8")], bias=GN_EPS)
            act(st8[:, :, 3], st8[:, :, 3], AF.Exp, [K2("st8")], [K2("st8")], scale=-0.5)
            tt("dve", yc, pY, st8[:, :, 0:1].to_broadcast([128, 8, 64]), ALU.subtract, [("ps", bY), K2("st8"), SK], [SK])
            tt("dve", yh, yc, st8[:, :, 3:4].to_broadcast([128, 8, 64]), ALU.mult, [SK, K2("st8")], [K2("yh")])
            bYT = next_ps()
            pYT = ps[bYT][:].bitcast(BF16)
            for par in range(2):
                for j in range(8):
                    P.add("pe", lambda e, j=j, par=par: e.transpose(
                        out=pYT[par * 64:(par + 1) * 64, j * 64:(j + 1) * 64], in_=yh[par * 64:(par + 1) * 64, j, :],
                        identity=identb[par * 64:(par + 1) * 64, par * 64:(par + 1) * 64]), reads=[K2("yh"), "identb"], writes=[("ps", bYT)])
            tt("dve", z1, v3(pYT[:, 0:512]), cT[:, :, R_GNG:R_GNG + 1].to_broadcast([128, 8, 64]), ALU.mult, [("ps", bYT), "cT"], [K2("z1")])
            tt("pool", z1, z1, cT[:, :, R_GNB:R_GNB + 1].to_broadcast([128, 8, 64]), ALU.add, [K2("z1"), "cT"], [K2("z1")])
            tt("pool", z1, z1, bonT, ALU.add, [K2("z1"), QK("bonT")], [K2("z1")])
            tt("pool", zT, z1, gT, ALU.mult, [K2("z1"), QK("gT")], [K2("zT")])
            lo = 16 if first else 0
            b = proj(Wo, ("A1", "Wo"), zT, K2("zT"))
            tt("dve", hT[:, :, tc0 + lo:tc0 + n], v3(ps[b][:])[:, :, lo:n], hT[:, :, tc0 + lo:tc0 + n], ALU.add, [("ps", b), HK], [HK])

        ringkeys = []
        for nm in ("r32", "k32", "kk32", "b32"):
            ringkeys.append((K2((nm, "j", 1)), 0))
        ringkeys.append((K2(("ew", "j", 1)), 1))
        for nm in ("Vtok", "gT", "bonT"):
            ringkeys.append((K2((nm, "q", 2)), 1))
        for k_, sl_ in ringkeys:
            r = P.reg.get(("ring", sl_))
            if r is not None:
                P.reg[k_] = [r[0], r[1], list(r[2])]
        L1, L2, L3 = [], [], []
        for ti in range(NTI):
            P.defer = []
            psr["lo"], psr["hi"] = 0, 3
            gen_P1(ti)
            L1.append(P.defer)
            P.defer = []
            psr["lo"], psr["hi"] = 3, 6
            gen_P2(ti)
            L2.append(P.defer)
            P.defer = []
            psr["lo"], psr["hi"] = 6, 8
            gen_S(ti)
            L3.append(P.defer)
        P.defer = None
        psr["lo"], psr["hi"] = 0, 8

        seq_ops = []
        for ti in range(NTI):
            seq_ops += L1[ti] + L2[ti] + L3[ti]
        ms = P.schedule(seq_ops, {"pe": 0.045, "act": 0.7, "dve": 0.7, "pool": 1.15, "sp": 2.0})
        if g == 0:
            print("rwkv list schedule: est makespan us", ms)
        for sl_ in range(2):
            joined = []
            for k_, s2 in ringkeys:
                if s2 == sl_:
                    r = P.reg.get(k_)
                    if r is not None:
                        if r[0] is not None:
                            joined.append((r[0], r[1]))
                        joined.extend(r[2])
            P.reg[("ring", sl_)] = [None, None, joined]
        joined = []
        for ti in range(NTI):
            r = P.reg.pop(("hT", ti), None)
            if r is not None:
                if r[0] is not None:
                    joined.append((r[0], r[1]))
                joined.extend(r[2])
        P.reg["hT"] = [None, None, joined]
        srow2 = scrS

        if g == 0:
            AR2.reset()
            Ss = AR2.take([128, 64, 64], F32); Tm = AR2.take([128, 64, 64], F32); T2 = AR2.take([128, 64, 64], F32)
            Xt = AR2.take([128, 8, 128], F32)
            Xp = [AR2.take([128, 128], F32) for _ in range(7)]
            sa = AR2.take([128, 64], F32); ys = AR2.take([128, 128], F32); y2 = AR2.take([128, 128], F32)
            s8 = AR2.take([128, 2, 8], F32); zp = AR2.take([128, 128], F32)
            Zt = AR2.take([128, 1024], F32); zTs = AR2.take([128, 8, 16], BF16)
            for h in range(2):
                b = next_ps()
                for jj in range(4):
                    j = h * 4 + jj
                    P.add("pe", lambda e, j=j, jj=jj, b=b: e.transpose(out=ps[b][0:112, jj * 128:(jj + 1) * 128], in_=sampF[:, j, :], identity=identf[:]),
                          reads=["sampF", "identf"], writes=[("ps", b)])
                ev("act", Xt[0:112, h * 4:h * 4 + 4, :], v3(ps[b][0:112, :], 4), [("ps", b)], [K2("Xt")])
            P.add("sp", lambda e: e.dma_start(out=scr1[:, :], in_=Xt[0:112, :, :].rearrange("p j e -> p (j e)")),
                  reads=[K2("Xt")], writes=["scr1"], dsem=21)
            for m in range(7):
                P.add("sp", lambda e, m=m: e.dma_start(out=Xp[m], in_=scr1[m * 16:(m + 1) * 16, :].rearrange("t (j e) -> (t j) e", e=128)),
                      reads=["scr1"], writes=[K2(("Xp", m))], dsem=17)
            XK = [K2(("Xp", m)) for m in range(7)] + ["CGp", "CGp", "CGp"]
            last = P.reg[K2(("Xp", 6))]
            for k_ in XK[:7]:
                P.reg[k_] = [last[0], last[1], []]
            CG = [CGp[:, ci, :] for ci in range(3)]
            bcv = lambda ap: ap.unsqueeze(1).to_broadcast([128, 64, 64])
            bck = lambda ap: ap.unsqueeze(2).to_broadcast([128, 64, 64])
            for par in range(2):
                cs_ = slice(par * 64, (par + 1) * 64)
                P.add("sp", lambda e, par=par: e.dma_start(out=Ss.rearrange("p v k -> p (v k)"), in_=swkv[:, par * 4096:(par + 1) * 4096]),
                      writes=[K2("Ss")], dsem=18)
                tt("dve", Tm, Ss, bcv(Xp[0][:, cs_]), ALU.mult, [K2("Ss"), XK[0]], [K2("Tm")])
                P.add("dve", lambda e: e.tensor_reduce(out=sa, in_=Tm, axis=AX.X, op=ALU.add), reads=[K2("Tm")], writes=[K2("sa")])
                tt("pool", T2, bck(Xp[3][:, cs_]), bcv(Xp[4][:, cs_]), ALU.mult, [XK[3], XK[4]], [K2("T2")])
                tt("dve", Ss, Ss, bcv(Xp[1][:, cs_]), ALU.mult, [K2("Ss"), XK[1]], [K2("Ss")])
                tt("pool", Tm, bck(sa), bcv(Xp[2][:, cs_]), ALU.mult, [K2("sa"), XK[2], K2("Tm")], [K2("Tm")])
                tt("pool", T2, T2, Tm, ALU.subtract, [K2("T2"), K2("Tm")], [K2("T2")])
                tt("dve", Ss, Ss, T2, ALU.add, [K2("Ss"), K2("T2")], [K2("Ss")])
                P.add("sp", lambda e, par=par: e.dma_start(out=o_wkvs[:, par * 4096:(par + 1) * 4096], in_=Ss.rearrange("p v k -> p (v k)")),
                      reads=[K2("Ss")], writes=[("o_wkvs", par)], dsem=19, final=True)
                tt("dve", Tm, Ss, bcv(Xp[5][:, cs_]), ALU.mult, [K2("Ss"), XK[5], K2("Tm")], [K2("Tm")])
                P.add("dve", lambda e, cs_=cs_: e.tensor_reduce(out=ys[:, cs_], in_=Tm, axis=AX.X, op=ALU.add), reads=[K2("Tm")], writes=[K2("ys")])
            y3 = ys.rearrange("p (a c) -> p a c", a=2)
            P.add("dve", lambda e: e.tensor_reduce(out=s8[:, :, 0], in_=y3, axis=AX.X, op=ALU.add), reads=[K2("ys")], writes=[K2("s8")])
            tt("dve", y2, ys, ys, ALU.mult, [K2("ys")], [K2("y2")])
            P.add("dve", lambda e: e.tensor_reduce(out=s8[:, :, 1], in_=y2.rearrange("p (a c) -> p a c", a=2), axis=AX.X, op=ALU.add), reads=[K2("y2"), K2("s8")], writes=[K2("s8")])
            P.add("dve", lambda e: e.tensor_scalar(out=s8[:, :, 0:2], in0=s8[:, :, 0:2], scalar1=1.0 / 64, scalar2=None, op0=ALU.mult), reads=[K2("s8")], writes=[K2("s8")])
            tt("dve", s8[:, :, 2], s8[:, :, 0], s8[:, :, 0], ALU.mult, [K2("s8")], [K2("s8")])
            tt("dve", s8[:, :, 3], s8[:, :, 1], s8[:, :, 2], ALU.subtract, [K2("s8")], [K2("s8")])
            act(s8[:, :, 3], s8[:, :, 3], AF.Sqrt, [K2("s8")], [K2("s8")], bias=GN_EPS)
            P.add("dve", lambda e: e.reciprocal(out=s8[:, :, 3], in_=s8[:, :, 3]), reads=[K2("s8")], writes=[K2("s8")])
            z3 = zp.rearrange("p (a c) -> p a c", a=2)
            tt("dve", z3, y3, s8[:, :, 0:1].to_broadcast([128, 2, 64]), ALU.subtract, [K2("ys"), K2("s8")], [K2("zp")])
            tt("dve", z3, z3, s8[:, :, 3:4].to_broadcast([128, 2, 64]), ALU.mult, [K2("zp"), K2("s8")], [K2("zp")])
            tt("dve", zp, zp, CG[0], ALU.mult, [K2("zp"), XK[7]], [K2("zp")])
            tt("dve", zp, zp, CG[1], ALU.add, [K2("zp"), XK[8]], [K2("zp")])
            tt("dve", y2, Xp[5], Xp[4], ALU.mult, [XK[5], XK[4], K2("y2")], [K2("y2")])
            tt("dve", y2, y2, CG[2], ALU.mult, [K2("y2"), XK[9]], [K2("y2")])
            P.add("dve", lambda e: e.tensor_reduce(out=s8[:, :, 4], in_=y2.rearrange("p (a c) -> p a c", a=2), axis=AX.X, op=ALU.add), reads=[K2("y2"), K2("s8")], writes=[K2("s8")])
            tt("dve", y2.rearrange("p (a c) -> p a c", a=2), Xp[3].rearrange("p (a c) -> p a c", a=2), s8[:, :, 4:5].to_broadcast([128, 2, 64]), ALU.mult,
               [XK[3], K2("s8"), K2("y2")], [K2("y2")])
            tt("dve", zp, zp, y2, ALU.add, [K2("zp"), K2("y2")], [K2("zp")])
            tt("dve", zp, zp, Xp[6], ALU.mult, [K2("zp"), XK[6]], [K2("zp")])
            P.add("sp", lambda e: e.dma_start(out=scr2[:, :], in_=zp), reads=[K2("zp")], writes=["scr2"], dsem=20)
            P.add("sp", lambda e: e.dma_start(out=Zt[0:16, :], in_=scr2.rearrange("(t j) e -> t (j e)", j=8)), reads=["scr2"], writes=[K2("Zt")], dsem=22)
            b = next_ps()
            for j in range(8):
                P.add("pe", lambda e, j=j, b=b: e.transpose(out=ps[b][:, j * 16:(j + 1) * 16], in_=Zt[0:16, j * 128:(j + 1) * 128], identity=identf[0:16, 0:16]),
                      reads=[K2("Zt"), "identf"], writes=[("ps", b)])
            ev("act", zTs, v3(ps[b][:, 0:128]), [("ps", b)], [K2("zTs")])
            b = next_ps()
            for o in range(8):
                for k in range(8):
                    P.add("pe", lambda e, o=o, k=k, b=b: e.matmul(ps[b][:, o * 16:(o + 1) * 16], lhsT=Wo[:, k, o * 128:(o + 1) * 128], rhs=zTs[:, k, :],
                                                                  start=(k == 0), stop=(k == 7)), reads=[("A1", "Wo"), K2("zTs")], writes=[("ps", b)])
            tt("dve", hT[:, :, 0:16], v3(ps[b][:, 0:128]), hT[:, :, 0:16], ALU.add, [("ps", b), "hT"], ["hT"])
        if g == NGRP - 1:
            for h in range(2):
                b = next_ps()
                for jj in range(4):
                    j = h * 4 + jj
                    P.add("pe", lambda e, j=j, jj=jj, b=b: e.transpose(out=ps[b][0:64, jj * 128:(jj + 1) * 128], in_=Sst[:, j, :], identity=identf[:]),
                          reads=["Sst", "identf"], writes=[("ps", b)])
                ev("act", scrS[0:64, h * 512:(h + 1) * 512], ps[b][0:64, :], [("ps", b)], [K2("scrS")])
            P.add("sp", lambda e: e.dma_start(out=o_wkvp.rearrange("h v k -> v h k"), in_=scrS[0:64, :].rearrange("p (h k) -> p h k", k=64)),
                  reads=[K2("scrS")], writes=["o_wkvp"], dsem=14, final=True)

    for g in range(NGRP):
        c0 = g * NG
        AR2.reset()
        xrow = [AR2.take([128, 1024], F32) for _ in range(2)]
        rows = []
        if g == 0:
            rows.append(("first", 64, 0))
            for i in range(5):
                rows.append((i * 128, 128, 64 + i * 128))
        else:
            r0 = c0 - 64
            for (o, m) in tiles_of(NG, 128):
                rows.append((r0 + o, m, o))
        for ti, (src, nr, dc) in enumerate(rows):
            xb = xrow[ti % 2]
            key = ("A2", "xrow", ti % 2)
            if src == "first":
                P.add("dve", lambda e, xb=xb: e.memset(xb[0:64, :], 0.0), writes=[key])
                P.add("sp", lambda e, xb=xb: e.dma_start(out=xb[0:16, :], in_=xs[:, :]), reads=[key], writes=[("A2", "xf0")], dsem=9)
                P.add("sp", lambda e, xb=xb: e.dma_start(out=xb[16:32, :], in_=sshift[:, :]), reads=[key], writes=[("A2", "xf1")], dsem=9)
                P.add("sp", lambda e, xb=xb: e.dma_start(out=xb[48:64, :], in_=meta[:, :]), reads=[key], writes=[("A2", "xf2")], dsem=9)
                r_ = P.reg[("A2", "xf2")]
                P.reg[key] = [r_[0], r_[1], []]
            else:
                P.add("sp", lambda e, xb=xb, src=src, nr=nr: e.dma_start(out=xb[0:nr, :], in_=xp[src:src + nr, :]), writes=[key], dsem=10 + ti % 2)
            for h in range(2):
                b = next_ps()
                for kk_ in range(4):
                    k = h * 4 + kk_
                    P.add("pe", lambda e, xb=xb, nr=nr, k=k, kk_=kk_, b=b: e.transpose(
                        out=ps[b][:, kk_ * 128:kk_ * 128 + nr], in_=xb[0:nr, k * 128:(k + 1) * 128], identity=identf[0:nr, 0:nr]),
                        reads=[key, "identf"], writes=[("ps", b)])
                P.add("act" if h == 0 else "dve",
                      (lambda e, h=h, b=b, nr=nr, dc=dc: e.activation(out=hT[:, h * 4:h * 4 + 4, dc:dc + nr],
                                                                      in_=ps[b][:].rearrange("p (k n) -> p k n", k=4)[:, :, 0:nr], func=AF.Copy))
                      if h == 0 else
                      (lambda e, h=h, b=b, nr=nr, dc=dc: e.tensor_copy(out=hT[:, h * 4:h * 4 + 4, dc:dc + nr],
                                                                       in_=ps[b][:].rearrange("p (k n) -> p k n", k=4)[:, :, 0:nr])),
                      reads=[("ps", b)], writes=["hT"])
        if g == 0:
            P.add("dve", lambda e: e.tensor_copy(out=shiftT[:], in_=hT[:, :, 16:32]), reads=["hT"], writes=["shiftT"])

        if stage >= 2:
            AR2.reset()
            AR1.reset()
            P.defer = []
            hnb = AR2.take([128, 8, NG], BF16)
            cbuf = AR2.take([128, 8, NG], F32)
            gluj = [AR2.take([128, 30 + NG], BF16) for _ in range(2)]
            dg = [AR2.take([128, 31, 128], BF16) for _ in range(2)]
            sg = [AR2.take([128, 352], F32) for _ in range(2)]
            rmsnorm(hT, NG, R_NMIX + 0, hnb, AR2, "A2", tilekeys=True)
            HN = lambda k, t0: ("A2", "dst", k, t0)
            for j in range(8):
                if j % 2 == 0:
                    q = j // 2
                    slot = ring_fill([
                        lambda r, q=q: (r[:].rearrange("p (k n) -> p k n", k=8)[:, :, 0:256], wv(w_pw1)[:, :, q * 256:(q + 1) * 256]),
                        lambda r, q=q: (r[:].rearrange("p (k n) -> p k n", k=8)[:, :, 256:512], wv(w_pw1)[:, :, D + q * 256:D + (q + 1) * 256])])
                W = ring[slot][:].rearrange("p (k n) -> p k n", k=8)
                gj = gluj[j % 2]
                gk = ("A2", "gluj", j % 2)
                P.add("pool", lambda e, gj=gj, j=j: e.tensor_copy(out=gj[:, 0:30], in_=gtail[:, j, :]), reads=["gtail"], writes=[gk])
                for ti, (t0, m) in enumerate(TL):
                    ba, bb = next_ps(), next_ps()
                    for k in range(8):
                        P.add("pe", lambda e, k=k, W=W, j=j, t0=t0, m=m, ba=ba: e.matmul(
                            ps[ba][:, 0:m], lhsT=W[:, k, (j % 2) * 128:(j % 2) * 128 + 128], rhs=hnb[:, k, t0:t0 + m], start=(k == 0), stop=(k == 7)),
                            reads=[("ring", slot), HN(k, t0)], writes=[("ps", ba)])
                    for k in range(8):
                        P.add("pe", lambda e, k=k, W=W, j=j, t0=t0, m=m, bb=bb: e.matmul(
                            ps[bb][:, 0:m], lhsT=W[:, k, 256 + (j % 2) * 128:256 + (j % 2) * 128 + 128], rhs=hnb[:, k, t0:t0 + m], start=(k == 0), stop=(k == 7)),
                            reads=[("ring", slot), ("ringg", slot), HN(k, t0)], writes=[("ps", bb)])
                    sgt = sg[ti % 2]
                    P.add("act", lambda e, sgt=sgt, bb=bb, m=m, j=j: e.activation(out=sgt[:, 0:m], in_=ps[bb][:, 0:m], func=AF.Sigmoid, bias=cv(R_BPW1 + 1, j)),
                          reads=[("ps", bb), "cT"], writes=[("A2", "sg", ti % 2)])
                    P.add("dve", lambda e, sgt=sgt, ba=ba, m=m, j=j, gj=gj, t0=t0: e.scalar_tensor_tensor(
                        out=gj[:, 30 + t0:30 + t0 + m], in0=ps[ba][:, 0:m], scalar=cv(R_BPW1, j), in1=sgt[:, 0:m], op0=ALU.add, op1=ALU.mult),
                        reads=[("ps", ba), ("A2", "sg", ti % 2), "cT"], writes=[gk])
                    if g == 0 and ti == 0:
                        P.add("dve", lambda e, sgt=sgt, ba=ba, j=j: e.scalar_tensor_tensor(
                            out=g32s[:, j, :], in0=ps[ba][:, 0:16], scalar=cv(R_BPW1, j), in1=sgt[:, 0:16], op0=ALU.add, op1=ALU.mult),
                            reads=[("ps", ba), ("A2", "sg", ti % 2), "cT"], writes=["g32s"])
                    if g == NGRP - 1 and ti == len(TL) - 1:
                        P.add("dve", lambda e, sgt=sgt, ba=ba, j=j, m=m: e.scalar_tensor_tensor(
                            out=g32p[:, j, :], in0=ps[ba][:, m - 30:m], scalar=cv(R_BPW1, j), in1=sgt[:, m - 30:m], op0=ALU.add, op1=ALU.mult),
                            reads=[("ps", ba), ("A2", "sg", ti % 2), "cT"], writes=["g32p"])
                if g == 0:
                    P.add("dve", lambda e, gj=gj: e.memset(gj[:, 30 + 16:30 + 48], 0.0), reads=[gk], writes=[gk])
                P.add("pool", lambda e, gj=gj, j=j: e.tensor_copy(out=gtail[:, j, :], in_=gj[:, NG:NG + 30]), reads=[gk], writes=["gtail"])
                dj = dg[j % 2]
                dk = ("A2", "dg", j % 2)
                P.add("dve", lambda e, dj=dj, j=j: e.tensor_tensor(
                    out=dj, in0=identb[:].unsqueeze(1).to_broadcast([128, 31, 128]),
                    in1=cT[:, j, R_WDW:R_WDW + 31].unsqueeze(2).to_broadcast([128, 31, 128]), op=ALU.mult),
                    reads=["identb", "cT"], writes=[dk], dur=4.5)
                for (t0, m) in TL:
                    b = next_ps()
                    for tap in range(31):
                        P.add("pe", lambda e, dj=dj, gj=gj, tap=tap, t0=t0, m=m, b=b: e.matmul(
                            ps[b][:, 0:m], lhsT=dj[:, tap, :], rhs=gj[:, t0 + tap:t0 + tap + m], start=(tap == 0), stop=(tap == 30)),
                            reads=[dk, gk], writes=[("ps", b)])
                    P.add("act", lambda e, b=b, m=m, t0=t0, j=j: e.activation(out=cbuf[:, j, t0:t0 + m], in_=ps[b][:, 0:m], func=AF.Identity, bias=cv(R_BDW, j)),
                          reads=[("ps", b), "cT"], writes=[("A2", "c", j, t0)])
            CKf = lambda k, t0: ("A2", "c", k, t0)
            CK = [CKf(k, 0) for k in range(8)]
            if g == 0:
                scT = AR1.take([128, 8, 480], F32)
                sctmp = AR1.take([128, 8, 480], F32)
                srow = [AR1.take([128, 1024], F32) for _ in range(2)]
                red = AR1.take([128, 8, 16], F32)
                for i in range(4):
                    nr = 128 if i < 3 else 96
                    sk = ("A1", "srow", i % 2)
                    P.add("sp", lambda e, i=i, nr=nr: e.dma_start(out=srow[i % 2][0:nr, :], in_=sconv[i * 128:i * 128 + nr, :]), writes=[sk], dsem=12 + i % 2)
                    for h in range(2):
                        b = next_ps()
                        for kk_ in range(4):
                            k = h * 4 + kk_
                            P.add("pe", lambda e, i=i, nr=nr, k=k, kk_=kk_, b=b: e.transpose(
                                out=ps[b][:, kk_ * 128:kk_ * 128 + nr], in_=srow[i % 2][0:nr, k * 128:(k + 1) * 128], identity=identf[0:nr, 0:nr]),
                                reads=[sk, "identf"], writes=[("ps", b)])
                        P.add("act", lambda e, h=h, b=b, nr=nr, i=i: e.activation(
                            out=scT[:, h * 4:h * 4 + 4, i * 128:i * 128 + nr], in_=ps[b][:].rearrange("p (k n) -> p k n", k=4)[:, :, 0:nr], func=AF.Copy),
                            reads=[("ps", b)], writes=[("A1", "scT")])
                P.add("sp", lambda e: e.dma_start(out=o_convs.rearrange("(t r) d -> t r d", r=30)[:, 0:29, :],
                                                  in_=sconv.rearrange("(t r) d -> t r d", r=30)[:, 1:30, :]), writes=["o_convs_a"], dsem=14, final=True)
                for k in range(8):
                    P.add("dve", lambda e, k=k: e.tensor_tensor(
                        out=sctmp[:, k, :].rearrange("p (t r) -> p t r", r=30), in0=scT[:, k, :].rearrange("p (t r) -> p t r", r=30),
                        in1=cT[:, k, R_WDW:R_WDW + 30].unsqueeze(1).to_broadcast([128, 16, 30]), op=ALU.mult),
                        reads=[("A1", "scT"), "cT"], writes=[("A1", "sctmp")])
                P.add("dve", lambda e: e.tensor_reduce(out=red, in_=sctmp.rearrange("p k (t r) -> p k t r", r=30), axis=AX.X, op=ALU.add),
                      reads=[("A1", "sctmp")], writes=[("A1", "red")])
                for k in range(8):
                    P.add("dve", lambda e, k=k: e.scalar_tensor_tensor(out=red[:, k, :], in0=g32s[:, k, :], scalar=cv(R_WDW + 30, k), in1=red[:, k, :],
                                                                       op0=ALU.mult, op1=ALU.add), reads=["g32s", ("A1", "red"), "cT"], writes=[("A1", "red")])
                    P.add("dve", lambda e, k=k: e.tensor_scalar(out=cbuf[:, k, 0:16], in0=red[:, k, :], scalar1=cv(R_BDW, k), scalar2=None, op0=ALU.add),
                          reads=[("A1", "red"), "cT"], writes=[CK[k]])
                for h in range(2):
                    b = next_ps()
                    for kk_ in range(4):
                        k = h * 4 + kk_
                        P.add("pe", lambda e, k=k, kk_=kk_, b=b: e.transpose(out=ps[b][0:16, kk_ * 128:(kk_ + 1) * 128], in_=g32s[:, k, :], identity=identf[:]),
                              reads=["g32s", "identf"], writes=[("ps", b)])
                    P.add("act", lambda e, h=h, b=b: e.activation(out=srow[0][0:16, h * 512:(h + 1) * 512], in_=ps[b][0:16, :], func=AF.Copy),
                          reads=[("ps", b)], writes=[("A1", "srow", 0)])
                P.add("sp", lambda e: e.dma_start(out=o_convs.rearrange("(t r) d -> t r d", r=30)[:, 29, :], in_=srow[0][0:16, :]),
                      reads=[("A1", "srow", 0)], writes=["o_convs_b"], dsem=14, final=True)
            if g == NGRP - 1:
                prow = AR1.take([128, 1024], F32)
                for h in range(2):
                    b = next_ps()
                    for kk_ in range(4):
                        k = h * 4 + kk_
                        P.add("pe", lambda e, k=k, kk_=kk_, b=b: e.transpose(out=ps[b][0:30, kk_ * 128:(kk_ + 1) * 128], in_=g32p[:, k, :], identity=identf[:]),
                              reads=["g32p", "identf"], writes=[("ps", b)])
                    P.add("act", lambda e, h=h, b=b: e.activation(out=prow[0:30, h * 512:(h + 1) * 512], in_=ps[b][0:30, :], func=AF.Copy),
                          reads=[("ps", b)], writes=[("A1", "prow")])
                P.add("sp", lambda e: e.dma_start(out=o_convp[:, :], in_=prow[0:30, :]), reads=[("A1", "prow")], writes=["o_convp"], dsem=14, final=True)
            sqb = [AR2.take([128, 352], BF16) for _ in range(2)]
            cbb = [AR2.take([128, 352], BF16) for _ in range(2)]
            mean = AR2.take([128, 352], F32)
            rstd = AR2.take([128, 352], F32)
            t1 = [AR2.take([128, 352], F32) for _ in range(2)]
            for (t0, m) in TL:
                b1, b2 = next_ps(), next_ps()
                for k in range(8):
                    P.add("act", lambda e, k=k, t0=t0, m=m: e.activation(out=sqb[k % 2][:, 0:m], in_=cbuf[:, k, t0:t0 + m], func=AF.Square),
                          reads=[CKf(k, t0)], writes=[("A2", "sqb", k % 2)])
                    P.add("dve", lambda e, k=k, t0=t0, m=m: e.tensor_copy(out=cbb[k % 2][:, 0:m], in_=cbuf[:, k, t0:t0 + m]),
                          reads=[CKf(k, t0)], writes=[("A2", "cbb", k % 2)])
                    P.add("pe", lambda e, k=k, m=m, b1=b1: e.matmul(ps[b1][:, 0:m], lhsT=onesb[:], rhs=cbb[k % 2][:, 0:m], start=(k == 0), stop=(k == 7)),
                          reads=[("A2", "cbb", k % 2), "onesb"], writes=[("ps", b1)])
                    P.add("pe", lambda e, k=k, m=m, b2=b2: e.matmul(ps[b2][:, 0:m], lhsT=onesb[:], rhs=sqb[k % 2][:, 0:m], start=(k == 0), stop=(k == 7)),
                          reads=[("A2", "sqb", k % 2), "onesb"], writes=[("ps", b2)])
                P.add("act", lambda e, m=m, b1=b1: e.activation(out=mean[:, 0:m], in_=ps[b1][:, 0:m], func=AF.Copy, scale=1.0 / D),
                      reads=[("ps", b1)], writes=[("A2", "mean")])
                P.add("dve", lambda e, m=m: e.tensor_tensor(out=rstd[:, 0:m], in0=mean[:, 0:m], in1=mean[:, 0:m], op=ALU.mult),
                      reads=[("A2", "mean")], writes=[("A2", "rstd")])
                P.add("dve", lambda e, m=m, b2=b2: e.scalar_tensor_tensor(out=rstd[:, 0:m], in0=ps[b2][:, 0:m], scalar=1.0 / D, in1=rstd[:, 0:m],
                                                                          op0=ALU.mult, op1=ALU.subtract), reads=[("ps", b2), ("A2", "rstd")], writes=[("A2", "rstd")])
                P.add("act", lambda e, m=m: e.activation(out=rstd[:, 0:m], in_=rstd[:, 0:m], func=AF.Sqrt, bias=LN_EPS), reads=[("A2", "rstd")], writes=[("A2", "rstd")])
                P.add("dve", lambda e, m=m: e.reciprocal(out=rstd[:, 0:m], in_=rstd[:, 0:m]), reads=[("A2", "rstd")], writes=[("A2", "rstd")])
                for k in range(8):
                    tt = t1[k % 2]
                    tk = ("A2", "t1", k % 2)
                    P.add("dve", lambda e, k=k, t0=t0, m=m, tt=tt: e.tensor_tensor(out=tt[:, 0:m], in0=cbuf[:, k, t0:t0 + m], in1=mean[:, 0:m], op=ALU.subtract),
                          reads=[CKf(k, t0), ("A2", "mean")], writes=[tk])
                    P.add("dve", lambda e, m=m, tt=tt: e.tensor_tensor(out=tt[:, 0:m], in0=tt[:, 0:m], in1=rstd[:, 0:m], op=ALU.mult),
                          reads=[tk, ("A2", "rstd")], writes=[tk])
                    P.add("act", lambda e, k=k, t0=t0, m=m, tt=tt: e.activation(out=hnb[:, k, t0:t0 + m], in_=tt[:, 0:m], func=AF.Silu,
                                                                                scale=cv(R_LNG, k), bias=cv(R_LNB, k)),
                          reads=[tk, "cT"], writes=[HN(k, t0)])
            for o in range(8):
                if o % 4 == 0:
                    hh = o // 4
                    slot = ring_fill([lambda r, hh=hh: (r[:].rearrange("p (k n) -> p k n", k=8), wv(w_pw2)[:, :, hh * 512:(hh + 1) * 512])])
                W = ring[slot][:].rearrange("p (k n) -> p k n", k=8)
                for (t0, m) in TL:
                    b = next_ps()
                    for k in range(8):
                        P.add("pe", lambda e, k=k, W=W, o=o, t0=t0, m=m, b=b: e.matmul(
                            ps[b][:, 0:m], lhsT=W[:, k, (o % 4) * 128:(o % 4) * 128 + 128], rhs=hnb[:, k, t0:t0 + m], start=(k == 0), stop=(k == 7)),
                            reads=[("ring", slot), HN(k, t0)], writes=[("ps", b)])
                    P.add("dve", lambda e, o=o, t0=t0, m=m, b=b: e.scalar_tensor_tensor(
                        out=hT[:, o, t0:t0 + m], in0=ps[b][:, 0:m], scalar=cv(R_BPW2, o), in1=hT[:, o, t0:t0 + m], op0=ALU.add, op1=ALU.add),
                        reads=[("ps", b), "hT", "cT"], writes=["hT"])

            opsB = P.defer
            P.defer = None
            msB = P.schedule(opsB, {"pe": 0.16, "act": 0.5, "dve": 0.5, "pool": 0.9, "sp": 2.0})
            if g == 0:
                print("phase B list schedule: est makespan us", msB)

        def mlp(l):
            AR2.reset()
            AR1.reset()
            hnb = AR2.take([128, 8, NG], BF16)
            hid = AR2.take([128, 32, NG], BF16)
            rl = [AR2.take([128, 352], F32) for _ in range(2)]
            wo = AR1.take([128, 32, 1024], BF16)
            rmsnorm(hT, NG, R_NMLP + l, hnb, AR2, "A2", tilekeys=True)
            HN = lambda k, t0: ("A2", "dst", k, t0)
            WOK = [("A1", "wo", q) for q in range(8)]
            for f in range(32):
                if f % 4 == 0:
                    q = f // 4
                    slot = ring_fill([lambda r, q=q: (r[:].rearrange("p (k n) -> p k n", k=8), wv(w_in[l])[:, :, q * 512:(q + 1) * 512])])
                    P.add("pool", lambda e, q=q: e.dma_start(out=wo[:, 4 * q:4 * q + 4, :], in_=w_out[l].rearrange("(f p) n -> p f n", p=128)[:, 4 * q:4 * q + 4, :]),
                          writes=[("A1", "wo", q)], dsem=40 + q)
                W = ring[slot][:].rearrange("p (k n) -> p k n", k=8)
                for ti, (t0, m) in enumerate(TL):
                    b = next_ps()
                    for k in range(8):
                        P.add("pe", lambda e, k=k, W=W, f=f, t0=t0, m=m, b=b: e.matmul(
                            ps[b][:, 0:m], lhsT=W[:, k, (f % 4) * 128:(f % 4) * 128 + 128], rhs=hnb[:, k, t0:t0 + m], start=(k == 0), stop=(k == 7)),
                            reads=[("ring", slot), HN(k, t0)], writes=[("ps", b)])
                    r_ = rl[ti % 2]
                    P.add("act", lambda e, r_=r_, b=b, m=m: e.activation(out=r_[:, 0:m], in_=ps[b][:, 0:m], func=AF.Relu),
                          reads=[("ps", b)], writes=[("A2", "rl", ti % 2)])
                    P.add("dve", lambda e, r_=r_, b=b, m=m, f=f, t0=t0: e.tensor_tensor(out=hid[:, f, t0:t0 + m], in0=ps[b][:, 0:m], in1=r_[:, 0:m], op=ALU.mult),
                          reads=[("ps", b), ("A2", "rl", ti % 2)], writes=[("A2", "hid", f)])
            for o in range(8):
                for (t0, m) in TL:
                    b = next_ps()
                    for f in range(32):
                        P.add("pe", lambda e, f=f, o=o, t0=t0, m=m, b=b: e.matmul(
                            ps[b][:, 0:m], lhsT=wo[:, f, o * 128:(o + 1) * 128], rhs=hid[:, f, t0:t0 + m], start=(f == 0), stop=(f == 31)),
                            reads=[WOK[f // 4], ("A2", "hid", f)], writes=[("ps", b)])
                    P.add("dve", lambda e, o=o, t0=t0, m=m, b=b: e.tensor_tensor(out=hT[:, o, t0:t0 + m], in0=ps[b][:, 0:m], in1=hT[:, o, t0:t0 + m], op=ALU.add),
                          reads=[("ps", b), "hT"], writes=["hT"])

        if stage >= 3:
            mlp(0)
        if stage >= 4:
            rwkv(g)
        if stage >= 5:
            mlp(1)

        AR2.reset()
        yfin = AR2.take([128, 8, NG], F32)
        yrow = [AR2.take([128, 1024], F32) for _ in range(2)]
        rmsnorm(hT, NG, R_NFIN, yfin, AR2, "A2", tilekeys=True)
        YKt = lambda k, c0_, m_: [("A2", "dst", k, t0) for (t0, mm) in TL if t0 < c0_ + m_ and c0_ < t0 + mm]
        outs = []
        if g == 0:
            outs.append((0, 16, y_s[:, :]))
            for i in range(5):
                outs.append((64 + i * 128, 128, y_p[i * 128:(i + 1) * 128, :]))
        else:
            r0 = c0 - 64
            for (o, m) in tiles_of(NG, 128):
                outs.append((o, m, y_p[r0 + o:r0 + o + m, :]))
        for oi, (col, m, dst) in enumerate(outs):
            yb = yrow[oi % 2]
            yk = ("A2", "yrow", oi % 2)
            for h in range(2):
                b = next_ps()
                for kk_ in range(4):
                    k = h * 4 + kk_
                    P.add("pe", lambda e, k=k, kk_=kk_, b=b, col=col, m=m: e.transpose(out=ps[b][0:m, kk_ * 128:(kk_ + 1) * 128], in_=yfin[:, k, col:col + m], identity=identf[:]),
                          reads=YKt(k, col, m) + ["identf"], writes=[("ps", b)])
                if h == 0:
                    P.add("act", lambda e, yb=yb, b=b, m=m: e.activation(out=yb[0:m, 0:512], in_=ps[b][0:m, :], func=AF.Copy), reads=[("ps", b)], writes=[yk])
                else:
                    P.add("dve", lambda e, yb=yb, b=b, m=m: e.tensor_copy(out=yb[0:m, 512:1024], in_=ps[b][0:m, :]), reads=[("ps", b)], writes=[yk])
            P.add("sp", lambda e, yb=yb, m=m, dst=dst: e.dma_start(out=dst, in_=yb[0:m, :]), reads=[yk], writes=[("y", g, oi)], dsem=15 + oi % 2, final=True)

    P.emit()
    st.close()
    return nc


_CACHE = {}


def _prep_inputs(inp):
    f = lambda a: np.ascontiguousarray(np.asarray(a, dtype=np.float32))
    vec_rows = [inp["norm_mix"][0], inp["norm_mix"][1], inp["norm_mlp"][0], inp["norm_mlp"][1], inp["norm_final"],
                inp["conv_b_pw1"][0][:D], inp["conv_b_pw1"][0][D:]]
    vec_rows += [inp["conv_w_dw"][0][j] for j in range(31)]
    vec_rows += [inp["conv_b_dw"][0], inp["conv_ln_g"][0], inp["conv_ln_b"][0], inp["conv_b_pw2"][0]]
    vec_rows += [inp["rwkv_x_mix"][0][m] for m in range(6)]
    vec_rows += [inp["rwkv_w0"][0], inp["rwkv_a0"][0], inp["rwkv_k_k"][0], inp["rwkv_k_a"][0],
                 np.asarray(inp["rwkv_r_k"][0]).reshape(-1), inp["rwkv_gn_g"][0], inp["rwkv_gn_b"][0]]
    vecs = f(np.stack([np.asarray(v, dtype=np.float32) for v in vec_rows], 0))
    shared = {
        "meta": f(inp["meta_tokens"]), "vecs": vecs,
        "w_pw1": f(inp["conv_w_pw1"][0]), "w_pw2": f(inp["conv_w_pw2"][0]),
        "w_r": f(inp["rwkv_w_r"][0]), "w_k": f(inp["rwkv_w_k"][0]), "w_v": f(inp["rwkv_w_v"][0]), "w_o": f(inp["rwkv_w_o"][0]),
        "w1": f(inp["rwkv_w1"][0]), "w2": f(inp["rwkv_w2"][0]), "a1": f(inp["rwkv_a1"][0]), "a2": f(inp["rwkv_a2"][0]),
        "g1": f(inp["rwkv_g1"][0]), "g2": f(inp["rwkv_g2"][0]),
        "w_in": f(inp["w_mlp_in"]), "w_out": f(inp["w_mlp_out"]),
    }
    maps = []
    for c in range(8):
        m = dict(shared)
        sl = slice(16 * c, 16 * c + 16)
        m["xp"] = f(inp["x_prompt"][c])
        m["xs"] = f(np.asarray(inp["x_sample"])[sl, 0, :])
        m["sconv"] = f(np.asarray(inp["state_conv"])[0, sl].reshape(480, D))
        m["sshift"] = f(np.asarray(inp["state_shift"])[0, sl])
        m["swkv"] = f(np.asarray(inp["state_wkv"])[0, sl].reshape(128, 8192))
        maps.append(m)
    return maps


def kernel(**inputs):
    stage = inputs.pop("_stage", 9)
    if stage not in _CACHE:
        _CACHE[stage] = build(stage)
    nc = _CACHE[stage]
    maps = _prep_inputs(inputs)
    res = run_bass_kernel_spmd(nc, maps, core_ids=list(range(8)))
    R = res.results
    y_prompt = np.stack([R[c]["y_p"] for c in range(8)], 0)
    y_sample = np.concatenate([R[c]["y_s"] for c in range(8)], 0).reshape(128, 1, D)
    conv_prompt = np.stack([R[c]["o_convp"] for c in range(8)], 0)[None]
    shift_prompt = np.stack([R[c]["o_shiftp"].reshape(D) for c in range(8)], 0)[None]
    wkv_prompt = np.stack([R[c]["o_wkvp"] for c in range(8)], 0)[None]
    conv_sample = np.concatenate([R[c]["o_convs"].reshape(16, 30, D) for c in range(8)], 0)[None]
    shift_sample = np.concatenate([R[c]["o_shifts"] for c in range(8)], 0)[None]
    wkv_sample = np.concatenate([R[c]["o_wkvs"].reshape(16, 16, 64, 64) for c in range(8)], 0)[None]
    return tuple(np.ascontiguousarray(a, dtype=np.float32) for a in
                 (y_prompt, y_sample, conv_prompt, shift_prompt, wkv_prompt, conv_sample, shift_sample, wkv_sample))
```
